# Optimizing a Trainium2 kernel written in Bass

```python
import math
import jax, jax.numpy as jnp
from jax import lax
import numpy as np

D_MODEL = 2048
BATCH = 1
SEQ = 8192
DEPTH = 1

HEAD_DIM = 128
N_Q_HEADS = 8
N_KV_HEADS = 2
GQA_GROUP = N_Q_HEADS // N_KV_HEADS
ATTN_WIDTH = N_Q_HEADS * HEAD_DIM
KV_WIDTH = N_KV_HEADS * HEAD_DIM
WINDOW = 128
BLOCK = 128
SPAN = BLOCK + 2 * WINDOW
N_BUCKETS = 32
MAX_DISTANCE = 128
POOL_SIZES = (2, 4, 8, 16)
N_POOL_GROUPS = len(POOL_SIZES)
POOL_WIDTH = D_MODEL // 2
POOL_GROUP_WIDTH = POOL_WIDTH // N_POOL_GROUPS
SPLIT_WIDTHS = (ATTN_WIDTH, KV_WIDTH, KV_WIDTH, ATTN_WIDTH, POOL_WIDTH, POOL_WIDTH)
SPLIT_POINTS = tuple(int(s) for s in np.cumsum(SPLIT_WIDTHS)[:-1])
IN_WIDTH = int(sum(SPLIT_WIDTHS))
N_BRANCHES = 2
EPS = 1e-6
NEG_INF = -1e30

kernel_name = "hybrid_swa_pool_gated_encoder"


def rmsnorm(x, g):
    xf = x.astype(jnp.float32)
    xf = xf * lax.rsqrt(jnp.mean(xf * xf, axis=-1, keepdims=True) + EPS)
    return xf.astype(x.dtype) * g


def t5_bucket(rel):
    half = N_BUCKETS // 2
    max_exact = half // 2
    ret = jnp.where(rel > 0, half, 0)
    n = jnp.abs(rel)
    nf = jnp.maximum(n, 1).astype(jnp.float32)
    large = max_exact + (jnp.log(nf / max_exact) / math.log(MAX_DISTANCE / max_exact)
                         * (half - max_exact)).astype(jnp.int32)
    large = jnp.minimum(large, half - 1)
    return ret + jnp.where(n < max_exact, n, large)


def windowed_gqa(q, k, v, rel_table, sink):
    B, S, _ = q.shape
    nblk = S // BLOCK
    qb = q.reshape(B, nblk, BLOCK, N_KV_HEADS, GQA_GROUP, HEAD_DIM)
    pad = ((0, 0), (WINDOW, WINDOW), (0, 0), (0, 0))
    kp = jnp.pad(k.reshape(B, S, N_KV_HEADS, HEAD_DIM), pad)
    vp = jnp.pad(v.reshape(B, S, N_KV_HEADS, HEAD_DIM), pad)
    idx = jnp.arange(nblk)[:, None] * BLOCK + jnp.arange(SPAN)[None, :]
    kw = kp[:, idx]
    vw = vp[:, idx]
    logits = jnp.einsum('bnqkgd,bntkd->bnkgqt', qb, kw).astype(jnp.float32) * (HEAD_DIM ** -0.5)
    rel = jnp.arange(SPAN)[None, :] - WINDOW - jnp.arange(BLOCK)[:, None]
    bias = rel_table[t5_bucket(rel)]
    bias = jnp.transpose(bias, (2, 0, 1)).reshape(N_KV_HEADS, GQA_GROUP, BLOCK, SPAN)
    key_pos = idx - WINDOW
    key_ok = (key_pos >= 0) & (key_pos < S)
    valid = (jnp.abs(rel) <= WINDOW)[None] & key_ok[:, None, :]
    logits = jnp.where(valid[None, :, None, None], logits + bias.astype(jnp.float32), NEG_INF)
    sink_l = jnp.broadcast_to(sink.astype(jnp.float32).reshape(N_KV_HEADS, GQA_GROUP, 1, 1),
                              logits.shape[:-1] + (1,))
    probs = jax.nn.softmax(jnp.concatenate([logits, sink_l], axis=-1), axis=-1)[..., :-1]
    out = jnp.einsum('bnkgqt,bntkd->bnqkgd', probs.astype(vw.dtype), vw)
    return out.reshape(B, S, ATTN_WIDTH)


def multiscale_pool(u, w_grp, scale):
    B, S, _ = u.shape
    ug = u.reshape(B, S, N_POOL_GROUPS, POOL_GROUP_WIDTH).astype(jnp.float32)
    cs = jnp.pad(jnp.cumsum(ug, axis=1), ((0, 0), (1, 0), (0, 0), (0, 0)))
    pos = jnp.arange(S)
    outs = []
    for gi, w in enumerate(POOL_SIZES):
        lo = jnp.clip(pos - w // 2, 0, S)
        hi = jnp.clip(pos + w // 2, 0, S)
        cnt = (hi - lo).astype(jnp.float32)[None, :, None]
        outs.append((cs[:, hi, gi] - cs[:, lo, gi]) / cnt - ug[:, :, gi])
    pooled = jnp.stack(outs, axis=2).astype(u.dtype)
    mixed = jnp.einsum('bsgc,gcd->bsgd', pooled, w_grp).reshape(B, S, POOL_WIDTH)
    return mixed * scale


def hybrid_layer(x, c, rel_table, w_ada, b_ada, pre_g, post_g, w_in, sink,
                 w_pool_grp, pool_scale, w_br_attn, w_br_pool, w_merge, b_merge, w_out):
    mod = jax.nn.silu(c) @ w_ada + b_ada
    shift, scale, gate = jnp.split(mod, 3, axis=-1)
    h = rmsnorm(x, pre_g) * (1.0 + scale[:, None]) + shift[:, None]
    proj = h @ w_in
    q, k, v, gate_a, u, gate_p = jnp.split(proj, SPLIT_POINTS, axis=-1)
    y_a = windowed_gqa(q, k, v, rel_table, sink) * jax.nn.silu(gate_a)
    y_p = multiscale_pool(u, w_pool_grp, pool_scale) * jax.nn.silu(gate_p)
    g = jax.nn.sigmoid(h @ w_merge + b_merge)
    g_a, g_p = jnp.split(g, N_BRANCHES, axis=-1)
    merged = g_a * (y_a @ w_br_attn) + g_p * (y_p @ w_br_pool)
    o = merged @ w_out
    return x + gate[:, None] * rmsnorm(o, post_g)


def setup_inputs(seed: int = 0) -> dict:
    key = jax.random.key(seed)
    ks = jax.random.split(key, 18)
    D = D_MODEL
    nrm = lambda k, shape, s: jax.random.normal(k, shape, jnp.float32) * s
    return {
        "x": nrm(ks[0], (BATCH, SEQ, D), 1.0),
        "c": nrm(ks[1], (BATCH, D), 1.0),
        "rel_bias_table": nrm(ks[2], (N_BUCKETS, N_Q_HEADS), 0.5),
        "w_ada": nrm(ks[3], (DEPTH, D, 3 * D), 0.1 * D ** -0.5),
        "b_ada": nrm(ks[4], (DEPTH, 3 * D), 0.02),
        "pre_norm_g": 1.0 + nrm(ks[5], (DEPTH, D), 0.05),
        "post_norm_g": 1.0 + nrm(ks[6], (DEPTH, D), 0.05),
        "w_in": nrm(ks[7], (DEPTH, D, IN_WIDTH), D ** -0.5),
        "attn_sink": nrm(ks[8], (DEPTH, N_Q_HEADS), 0.5),
        "w_pool_group": nrm(ks[9], (DEPTH, N_POOL_GROUPS, POOL_GROUP_WIDTH, POOL_GROUP_WIDTH),
                            POOL_GROUP_WIDTH ** -0.5),
        "pool_scale": 1.0 + nrm(ks[10], (DEPTH, POOL_WIDTH), 0.1),
        "w_branch_attn": nrm(ks[11], (DEPTH, ATTN_WIDTH, D), ATTN_WIDTH ** -0.5),
        "w_branch_pool": nrm(ks[12], (DEPTH, POOL_WIDTH, D), POOL_WIDTH ** -0.5),
        "w_merge": nrm(ks[13], (DEPTH, D, N_BRANCHES * D), D ** -0.5),
        "b_merge": nrm(ks[14], (DEPTH, N_BRANCHES * D), 0.02),
        "w_out": nrm(ks[15], (DEPTH, D, D), D ** -0.5),
    }


def reference(x, c, rel_bias_table, w_ada, b_ada, pre_norm_g, post_norm_g, w_in, attn_sink,
              w_pool_group, pool_scale, w_branch_attn, w_branch_pool, w_merge, b_merge, w_out):
    for l in range(DEPTH):
        x = hybrid_layer(x, c, rel_bias_table, w_ada[l], b_ada[l], pre_norm_g[l], post_norm_g[l],
                         w_in[l], attn_sink[l], w_pool_group[l], pool_scale[l],
                         w_branch_attn[l], w_branch_pool[l], w_merge[l], b_merge[l], w_out[l])
    return x
```

```python
import math
import os as _os
from contextlib import ExitStack

import numpy as np
import concourse.bass as bass
import concourse.mybir as mybir
from concourse.bass_utils import run_bass_kernel_spmd

F32, BF16, F32R = mybir.dt.float32, mybir.dt.bfloat16, mybir.dt.float32r
AF = mybir.ActivationFunctionType
ALU = mybir.AluOpType
AX = mybir.AxisListType

NCORES = 8
S_TOT, D = 8192, 2048
T = S_TOT // NCORES
NPOS = 10
EPS = 1e-6
NEG = -30000.0
IN_W = 4608
QS = 128 ** -0.5
ENGS = ("sp", "act", "pool", "dve", "pe")
NRING = 16


class Op:
    __slots__ = ("eng", "fn", "deps", "dma", "sem", "val", "users", "name")

    def __init__(self, eng, fn, dma, name):
        self.eng, self.fn, self.dma, self.name = eng, fn, dma, name
        self.deps = set()
        self.sem = None
        self.val = 0
        self.users = 0


class Rec:
    __slots__ = ("lo", "hi", "w", "r")

    def __init__(self, lo, hi, w):
        self.lo, self.hi, self.w, self.r = lo, hi, w, {}


class Sched:
    def __init__(self):
        self.q = {e: [] for e in ENGS}
        self.recs = {"sb": [], "ps": []}
        self.ndma = {e: [] for e in ENGS}

    @staticmethod
    def _ordered(d, o):
        return (not d.dma) and (not o.dma) and d.eng == o.eng and d.eng == "pe"

    def op(self, eng, fn, reads=(), writes=(), dma=False, name=""):
        o = Op(eng, fn, dma, name)
        deps = set()
        reads = [(sp, lo // 2048 * 2048, -(-hi // 2048) * 2048) if sp == "ps" else (sp, lo, hi)
                 for (sp, lo, hi) in reads]
        writes = [(sp, lo // 2048 * 2048, -(-hi // 2048) * 2048) if sp == "ps" else (sp, lo, hi)
                  for (sp, lo, hi) in writes]
        for (sp, lo, hi) in reads:
            for r in self.recs[sp]:
                if r.lo < hi and lo < r.hi:
                    if r.w is not None and r.w is not o:
                        d = r.w
                        if not (d.eng == "pe" and eng == "pe" and not dma):
                            deps.add(d)
                    if sp == "ps":
                        for x in r.r.values():
                            if x is not o and x.eng != eng:
                                deps.add(x)
                    r.r[id(o) if dma else eng] = o
        for (sp, lo, hi) in writes:
            new = []
            for r in self.recs[sp]:
                if r.lo < hi and lo < r.hi:
                    if r.w is not None and r.w is not o and not self._ordered(r.w, o):
                        deps.add(r.w)
                    for x in r.r.values():
                        if x is not o and not self._ordered(x, o):
                            deps.add(x)
                    if lo <= r.lo and r.hi <= hi:
                        continue
                new.append(r)
            new.append(Rec(lo, hi, o))
            self.recs[sp] = new
        if dma:
            ring = self.ndma[eng]
            if len(ring) >= NRING:
                deps.add(ring[len(ring) - NRING])
            ring.append(o)
        deps.discard(o)
        o.deps = deps
        for d in deps:
            d.users += 1
        self.q[eng].append(o)
        return o

    def emit(self, nc, es):
        engsem = {e: es.enter_context(nc.semaphore("s_" + e)) for e in ENGS}
        rings = {e: [es.enter_context(nc.semaphore("d_%s%d" % (e, i))) for i in range(NRING)]
                 for e in ENGS if self.ndma[e]}
        for e in ENGS:
            cnt = 0
            for o in self.q[e]:
                if o.dma:
                    continue
                if o.users > 0:
                    cnt += 1
                    o.sem, o.val = engsem[e], cnt
        for e in ENGS:
            for i, o in enumerate(self.ndma[e]):
                o.sem, o.val = rings[e][i % NRING], 16 * (i // NRING + 1)

        def run(e, eng):
            waited = {}
            for o in self.q[e]:
                need = {}
                for d in o.deps:
                    k = id(d.sem)
                    if k not in need or need[k][1] < d.val:
                        need[k] = (d.sem, d.val)
                for k, (sem, val) in need.items():
                    if waited.get(k, 0) >= val:
                        continue
                    waited[k] = val
                    eng.wait_ge(sem, val)
                ins = o.fn(eng)
                if o.dma:
                    ins.then_inc(o.sem, 16)
                elif o.users > 0:
                    ins.then_inc(o.sem, 1)
            for i, sem in enumerate(rings.get(e, [])):
                n = len(self.ndma[e])
                uses = (n - i + NRING - 1) // NRING if n > i else 0
                if uses > 0:
                    eng.wait_ge(sem, 16 * uses)

        block = es.enter_context(nc.Block())
        block_engines = {"sp": block.sync, "act": block.scalar, "pool": block.gpsimd, "dve": block.vector,
                         "pe": block.tensor}
        for e in ENGS:
            block_engines[e](lambda eng, e=e: run(e, eng))


class Buf:
    def __init__(self, space, ap, lo, hi):
        self.space, self.ap, self.lo, self.hi = space, ap, lo, hi

    @property
    def rng(self):
        return (self.space, self.lo, self.hi)

    def __getitem__(self, k):
        return self.ap[k]


def build_program(debug=(), stop=None):
    nc = bass.Bass("TRN2", target_bir_lowering=False)

    def din(name, shape):
        return nc.dram_tensor(name, list(shape), F32, kind="ExternalInput").ap()

    xh = din("xh", [T + 256, D])
    cT_d = din("cT", [128, 16])
    w_ada = din("w_ada", [D, 3 * D])
    b_ada = din("b_ada", [1, 3 * D])
    pre_g = din("pre_g", [1, D])
    post_g = din("post_g", [1, D])
    w_in = din("w_in", [D, IN_W])
    w_pool = din("w_pool", [4, 256, 256])
    w_a = din("w_a", [1024, D])
    w_p = din("w_p", [1024, D])
    w_m = din("w_m", [D, 2 * D])
    w_out = din("w_out", [D, D])
    biasT = din("biasT", [128, 5 * 8 * 128])
    bmT_d = din("bmT", [128, 32])
    pscT_d = din("pscT", [128, 8])
    sinkB_d = din("sinkB", [128, 8])
    pflag_d = din("pflag", [128, 2])
    cinv_d = din("cinv", [128, 64])
    ident_d = din("identf", [128, 128])
    out_d = nc.dram_tensor("out", [T, D], F32, kind="ExternalOutput").ap()
    dbg_d = {}
    for (nm, shp, dt_) in debug:
        dbg_d[nm] = nc.dram_tensor("dbg_" + nm, list(shp), dt_, kind="ExternalOutput").ap()

    S = Sched()
    es = ExitStack()
    ARENA_B = 207872
    arena = es.enter_context(nc.sbuf_tensor("arena", [128, ARENA_B // 4], F32))
    psum = es.enter_context(nc.psum_tensor("psum", [128, 4096], F32))

    def sb(off, dtype, shape):
        n = int(np.prod(shape))
        esz = 2 if dtype == BF16 else 4
        assert off % 4 == 0 and (n * esz) % 4 == 0
        ap = arena[:, off // 4: off // 4 + (n * esz) // 4]
        if dtype != F32:
            ap = ap.bitcast(dtype)
        if len(shape) == 2:
            ap = ap.rearrange("p (a b) -> p a b", a=shape[0])
        elif len(shape) == 3:
            ap = ap.rearrange("p (a b c) -> p a b c", a=shape[0], b=shape[1])
        assert off + n * esz <= ARENA_B
        return Buf("sb", ap, off, off + n * esz)

    def sub(buf, lo, hi):
        return (buf.space, buf.lo + lo, buf.lo + hi)

    def bank(k, dtype=F32):
        ap = psum[:, 512 * k: 512 * (k + 1)]
        if dtype != F32:
            ap = ap.bitcast(dtype)
        return Buf("ps", ap, 2048 * k, 2048 * (k + 1))

    banks = [bank(k) for k in range(8)]

    A0, B0, W0, M0, X0, C0 = 0, 40960, 61440, 110592, 135168, 194560
    hT = sb(A0, BF16, (NPOS, 16, 128))
    hT_flat = sb(A0, BF16, (NPOS * 16 * 128,))

    def hT_rng(p, half=None):
        if half is None:
            return sub(hT, p * 4096, (p + 1) * 4096)
        return sub(hT, p * 4096 + half * 2048, p * 4096 + (half + 1) * 2048)

    biasm = sb(B0, F32, (5, 8, 128))
    postgB = sb(B0, F32, (D,))
    modS = sb(M0, F32, (D,))
    modG = sb(M0 + 8192, F32, (D,))
    modGP = sb(M0 + 16384, F32, (D,))
    modB = sb(M0, F32, (3 * D,))
    yaT = sb(M0, BF16, (8, 8, 128))
    sk_f = sb(M0, F32, (1024,))
    scB = sb(X0, BF16, (16, 128))
    xt = [sb(X0 + 8192, F32, (D,)), sb(X0 + 16384, F32, (D,)), sb(B0 + 8192, F32, (D,)), sb(X0 + 45056, F32, (D,))]
    tbuf = sb(X0 + 24576, F32, (D,))
    hb = [sb(X0 + 32768, BF16, (D,)), sb(X0 + 36864, BF16, (D,))]
    junkA2 = [sb(X0 + 40960, BF16, (D,)), sb(X0 + 53248, BF16, (D,))]
    pregB = sb(X0 + 45056, F32, (D,))
    ada_slot = [sb(W0 + 8192 * i, BF16, (16, 256)) for i in range(6)]
    qT = sb(X0, BF16, (8, 8, 128))
    sga = sb(X0 + 16384, BF16, (8, 8, 128))
    kT = sb(X0 + 32768, BF16, (NPOS, 2, 128))
    Vt = sb(X0 + 37888, BF16, (NPOS, 256))
    scb = [sb(X0 + 43008 + 2048 * i, F32, (4, 128)) for i in range(3)]
    PT = [[sb(X0 + 49152 + 3072 * j + 1024 * c, BF16, (4, 128)) for c in range(3)] for j in range(2)]
    den = sb(X0 + 55296, F32, (512,))
    den2 = [den, sb(X0 + 57344, F32, (512,))]
    osb = sb(X0 + 57344, F32, (4, 128))
    ypT = sb(X0, BF16, (8, 8, 128))
    ub = [sb(X0 + 16384, F32, (1040,)), sb(X0 + 20544, F32, (1040,))]
    pa = sb(X0 + 24704, F32, (1040,))
    pb = sb(X0 + 28864, F32, (1040,))
    sgp = sb(X0 + 33024, BF16, (2, 1024))
    pooled = sb(X0 + 37120, BF16, (2, 1024))
    etmp = sb(X0 + 41216, F32, (16,))
    mergedT = sb(X0 + 16384, BF16, (8, 16, 128))
    siga = sb(B0, F32, (1024,))
    sigp = sb(B0 + 4096, F32, (1024,))
    t1 = sb(B0 + 8192, F32, (1024,))
    t2 = sb(B0 + 12288, F32, (1024,))
    obuf = [sb(A0 + 8192 * i, F32, (D,)) for i in range(5)] + \
           [sb(B0, F32, (D,)), sb(B0 + 8192, F32, (D,)), sb(X0 + 49152, F32, (D,))]
    xf = [sb(M0, F32, (D,)), sb(M0 + 8192, F32, (D,))]
    junkF2 = [sb(X0 + 57344, BF16, (512,)), sb(X0 + 58368, BF16, (512,))]
    c = C0
    def calloc(dtype, shape):
        nonlocal c
        b = sb(c, dtype, shape)
        c = b.hi
        c = (c + 31) // 32 * 32
        return b
    ident = calloc(BF16, (128,))
    onesb = calloc(BF16, (128,))
    onesf = calloc(F32, (128,))
    identf = calloc(F32, (128,))
    cT = calloc(F32, (16,))
    scs = calloc(F32, (16,))
    bmT = calloc(F32, (32,))
    pscT = calloc(F32, (8,))
    sinkB = calloc(F32, (8,))
    expsink = calloc(F32, (8,))
    pflag = calloc(F32, (2,))
    cinv = calloc(F32, (4, 16))
    ss = calloc(F32, (16,))
    rstd = calloc(F32, (16,))
    ssf = calloc(F32, (32,))
    sst = calloc(F32, (8,))
    rstdf = calloc(F32, (8,))
    epst = calloc(F32, (2,))
    wpool = calloc(BF16, (4, 2, 256))
    sk_hi = calloc(BF16, (1024,))
    sk_lo = calloc(BF16, (1024,))
    wslot = [(W0 + 8192 * i) for i in range(6)]

    def DMA(eng, out_buf_rng, out_ap, in_ap, reads=(), name="dma"):
        return S.op(eng, lambda e: e.dma_start(out=out_ap, in_=in_ap), reads=list(reads),
                    writes=[out_buf_rng] if out_buf_rng is not None else [], dma=True, name=name)

    def dbg_dump(nm, buf, ap2d):
        if nm in dbg_d:
            DMA("sp", None, dbg_d[nm], ap2d, reads=[buf.rng], name="dbg")

    def finish_early():
        DMA("sp", None, out_d[0:128, :], modB[:, 0:D], reads=[modB.rng], name="out")
        S.emit(nc, es)
        es.close()
        return nc

    DMA("sp", identf.rng, identf.ap, ident_d)
    DMA("sp", cT.rng, cT.ap, cT_d)
    S.op("dve", lambda e: e.memset(onesf.ap, 1.0), writes=[onesf.rng])
    S.op("dve", lambda e: e.memset(onesb.ap, 1.0), writes=[onesb.rng])
    S.op("dve", lambda e: e.memset(ss.ap, 0.0), writes=[ss.rng])
    S.op("dve", lambda e: e.memset(ssf.ap, 0.0), writes=[ssf.rng])
    S.op("dve", lambda e: e.memset(epst.ap, EPS), writes=[epst.rng])
    S.op("dve", lambda e: e.tensor_copy(out=ident.ap, in_=identf.ap), reads=[identf.rng], writes=[ident.rng])
    S.op("act", lambda e: e.activation(out=scs.ap, in_=cT.ap, func=AF.Silu), reads=[cT.rng], writes=[scs.rng])
    scB_r = scB.ap
    for kc in range(16):
        S.op("dve", lambda e, kc=kc: e.tensor_scalar(out=scB_r[:, kc, :], in0=onesf.ap, scalar1=scs[:, kc:kc + 1],
                                                      scalar2=None, op0=ALU.mult),
             reads=[onesf.rng, scs.rng], writes=[sub(scB, kc * 256, (kc + 1) * 256)])
    def xrow(p):
        r = 0 if p == 0 else (9 if p == 1 else p - 1)
        return xh[r * 128:(r + 1) * 128, :]
    for p in range(3):
        DMA("sp", xt[p].rng, xt[p].ap, xrow(p), name="x")
    for j in range(3):
        DMA("sp", sub(modB, j * 8192, (j + 1) * 8192), modB[:, j * D:(j + 1) * D],
            b_ada[0:1, j * D:(j + 1) * D].partition_broadcast(128))
    DMA("sp", pregB.rng, pregB.ap, pre_g[0:1, :].partition_broadcast(128))
    DMA("sp", postgB.rng, postgB.ap, post_g[0:1, :].partition_broadcast(128))
    for (bf, dd) in ((bmT, bmT_d), (pscT, pscT_d), (sinkB, sinkB_d), (pflag, pflag_d)):
        DMA("sp", bf.rng, bf.ap, dd)
    DMA("sp", cinv.rng, cinv.ap, cinv_d.rearrange("p (a b) -> p a b", a=4))
    S.op("act", lambda e: e.activation(out=expsink.ap, in_=sinkB.ap, func=AF.Exp), reads=[sinkB.rng],
         writes=[expsink.rng])

    tiles = []
    state = {"slot": 0, "issued": 0}
    tenant = {}

    def add_tile(src, kc, ncols, off_in_slot=0, newslot=True, nslots=1, fixed=None):
        if fixed is not None:
            base_, prev_ = fixed
            b = sb(base_, BF16, (kc, ncols))
            tiles.append((src, b, prev_))
            return len(tiles) - 1
        if newslot:
            if nslots == 2 and state["slot"] % 2 == 1:
                state["slot"] += 1
            first = state["slot"] % 6
            state["slot"] += nslots
        else:
            first = (state["slot"] - 1) % 6
        base = wslot[first]
        b = sb(base + off_in_slot, BF16, (kc, ncols))
        idx = len(tiles)
        prev = -1
        for sl_ in range(first, first + nslots):
            cur, old = tenant.get(sl_, ([], -1))
            if newslot:
                p_ = max(cur) if cur else -1
                tenant[sl_] = ([idx], p_)
                prev = max(prev, p_)
            else:
                cur.append(idx)
                prev = max(prev, old)
        tiles.append((src, b, prev))
        return idx

    def issue_upto(n_last, n_first=None):
        if n_first is None:
            n_first = n_last
        lim = min(n_last + LA, len(tiles) - 1)
        while state["issued"] <= lim:
            src, b, prev = tiles[state["issued"]]
            if state["issued"] > n_last and prev >= n_first:
                break
            assert prev < n_first
            DMA("pool", b.rng, b.ap, src, name="w%d" % state["issued"])
            state["issued"] += 1
        assert state["issued"] > n_last

    w_in_v = w_in.rearrange("(kc p) n -> p kc n", p=128)
    w_m_v = w_m.rearrange("(kc p) n -> p kc n", p=128)
    w_a_v = w_a.rearrange("(kc p) n -> p kc n", p=128)
    w_p_v = w_p.rearrange("(kc p) n -> p kc n", p=128)
    w_out_v = w_out.rearrange("(kc p) n -> p kc n", p=128)

    def win_tile(c0):
        return add_tile(w_in_v[:, :, c0:c0 + 256], 16, 256)

    T_k = win_tile(1024)
    T_v = win_tile(1280)
    w_ada_v = w_ada.rearrange("(kc p) n -> p kc n", p=128)
    T_gate = [add_tile(w_ada_v[:, :, 2 * D + 256 * i: 2 * D + 256 * (i + 1)], 16, 256) for i in range(8)]
    T_q = [win_tile(256 * i) for i in range(4)]
    T_ga = [win_tile(1536 + 256 * i) for i in range(4)]
    T_u, T_gp = [], []
    for g in range(4):
        T_u.append(win_tile(2560 + 256 * g))
        T_gp.append(win_tile(3584 + 256 * g))
    T_e = []
    for st in range(8):
        c0 = st * 256
        ta = add_tile(w_a_v[:, :, c0:c0 + 256], 8, 256)
        tp = add_tile(w_p_v[:, :, c0:c0 + 256], 8, 256, off_in_slot=4096, newslot=False)
        tma = add_tile(w_m_v[:, :, c0:c0 + 256], 16, 256)
        tmp_ = add_tile(w_m_v[:, :, D + c0:D + c0 + 256], 16, 256)
        T_e.append((ta, tp, tma, tmp_))
    T_o = [add_tile(w_out_v[:, :, n * 512:(n + 1) * 512], 16, 512, nslots=2) for n in range(3)]
    T_o.append(add_tile(w_out_v[:, :, 3 * 512:4 * 512], 16, 512, fixed=(X0, T_e[-1][3])))
    LA = 4


    def issue_cap(k):
        while state["issued"] <= k:
            src, b, prev = tiles[state["issued"]]
            assert prev < 0
            DMA("pool", b.rng, b.ap, src, name="w%d" % state["issued"])
            state["issued"] += 1

    if stop == "setup":
        return finish_early()
    w_ada_v = w_ada.rearrange("(kc p) n -> p kc n", p=128)

    def p0_dma(j):
        sl = ada_slot[j % 6]
        DMA("pool", sl.rng, sl.ap, w_ada_v[:, :, j * 256:(j + 1) * 256], name="wada")

    def p0_consume(j):
        sl = ada_slot[j % 6]
        bk = banks[j % 2]
        for kc in range(16):
            S.op("pe", lambda e, kc=kc: e.matmul(bk[:, 0:256], lhsT=scB_r[:, kc, :], rhs=sl[:, kc, :],
                                                  start=(kc == 0), stop=(kc == 15)),
                 reads=[sl.rng, scB.rng], writes=[bk.rng])
        S.op("dve", lambda e: e.tensor_tensor(out=modB[:, j * 256:(j + 1) * 256], in0=bk[:, 0:256],
                                              in1=modB[:, j * 256:(j + 1) * 256], op=ALU.add),
             reads=[bk.rng, sub(modB, j * 1024, (j + 1) * 1024)],
             writes=[sub(modB, j * 1024, (j + 1) * 1024)])

    for j in range(6):
        p0_dma(j)
    for j in range(16):
        p0_consume(j)
        if j + 6 <= 15:
            p0_dma(j + 6)
    issue_cap(3)
    S.op("dve", lambda e: e.scalar_tensor_tensor(out=modG.ap, in0=modG.ap, scalar=1.0, in1=pregB.ap,
                                                 op0=ALU.add, op1=ALU.mult),
         reads=[modG.rng, pregB.rng], writes=[modG.rng])

    if stop == "p0":
        return finish_early()
    def pa_s1(p):
        x = xt[p % 4]
        if p >= 3:
            DMA("sp", x.rng, x.ap, xrow(p), name="x")
        junkA = junkA2[p % 2]
        S.op("act", lambda e: e.activation(out=junkA.ap, in_=x.ap, func=AF.Square, accum_out=ss[:, p:p + 1]),
             reads=[x.rng], writes=[junkA.rng, sub(ss, 4 * p, 4 * p + 4)])
        S.op("act", lambda e: e.activation(out=rstd[:, p:p + 1], in_=ss[:, p:p + 1], func=AF.Sqrt,
                                           bias=epst[:, 0:1], scale=1.0 / D),
             reads=[sub(ss, 4 * p, 4 * p + 4), epst.rng], writes=[sub(rstd, 4 * p, 4 * p + 4)])
        S.op("dve", lambda e: e.reciprocal(out=rstd[:, p:p + 1], in_=rstd[:, p:p + 1]),
             reads=[sub(rstd, 4 * p, 4 * p + 4)], writes=[sub(rstd, 4 * p, 4 * p + 4)])
        S.op("dve", lambda e: e.scalar_tensor_tensor(out=tbuf.ap, in0=x.ap, scalar=rstd[:, p:p + 1], in1=modG.ap,
                                                     op0=ALU.mult, op1=ALU.mult),
             reads=[x.rng, sub(rstd, 4 * p, 4 * p + 4), modG.rng], writes=[tbuf.rng])
        h = hb[p % 2]
        S.op("dve", lambda e: e.tensor_tensor(out=h[:, 0:1280], in0=tbuf[:, 0:1280], in1=modS[:, 0:1280], op=ALU.add),
             reads=[sub(tbuf, 0, 5120), sub(modS, 0, 5120)], writes=[sub(h, 0, 2560)])
        S.op("pool", lambda e: e.tensor_tensor(out=h[:, 1280:2048], in0=tbuf[:, 1280:2048], in1=modS[:, 1280:2048],
                                               op=ALU.add),
             reads=[sub(tbuf, 5120, 8192), sub(modS, 5120, 8192)], writes=[sub(h, 2560, 4096)])

    def pa_s2(p):
        h = hb[p % 2]
        k0 = 2 + 2 * (p % 2)
        pT = psum[:, 512 * k0: 512 * (k0 + 2)].bitcast(BF16)
        prng = ("ps", 2048 * k0, 2048 * (k0 + 2))
        for kc in range(16):
            S.op("pe", lambda e, kc=kc: e.transpose(pT[:, kc * 128:(kc + 1) * 128], h[:, kc * 128:(kc + 1) * 128],
                                                    ident.ap),
                 reads=[sub(h, kc * 256, (kc + 1) * 256), ident.rng], writes=[prng])
        S.op("act", lambda e: e.activation(out=hT[:, p, 0:8, :],
                                           in_=pT[:, 0:1024].rearrange("p (a b) -> p a b", a=8), func=AF.Copy),
             reads=[("ps", 2048 * k0, 2048 * (k0 + 1))], writes=[hT_rng(p, 0)])
        S.op("dve", lambda e: e.tensor_copy(out=hT[:, p, 8:16, :],
                                            in_=pT[:, 1024:2048].rearrange("p (a b) -> p a b", a=8)),
             reads=[("ps", 2048 * (k0 + 1), 2048 * (k0 + 2))], writes=[hT_rng(p, 1)])

    pa_s1(0)
    for p in range(NPOS):
        if p + 1 < NPOS:
            pa_s1(p + 1)
        pa_s2(p)
    dbg_dump("hT", hT, hT_flat.ap)

    if stop == "pa":
        return finish_early()
    DMA("pool", wpool.rng, wpool.ap, w_pool.rearrange("g (cc p) d -> p g cc d", p=128), name="wpool")

    bank_ctr = {"n": 0}

    def nbank():
        b = banks[bank_ctr["n"] % 8]
        bank_ctr["n"] += 1
        return b

    MAIN = (slice(2, 6), slice(6, 10))

    def fm_mm(ti, cc, kcs, rhs_fn, rhs_rngs, bks, widths):
        wb = tiles[ti][1]
        for kc in range(kcs):
            for j, bk in enumerate(bks):
                S.op("pe", lambda e, kc=kc, j=j, bk=bk: e.matmul(bk[:, 0:widths[j]],
                                                                  lhsT=wb[:, kc, cc * 128:(cc + 1) * 128],
                                                                  rhs=rhs_fn(j, kc),
                                                                  start=(kc == 0), stop=(kc == kcs - 1)),
                     reads=[wb.rng] + rhs_rngs[j], writes=[sub(bk, 0, widths[j] * 4)])

    def hT_rhs(j, kc):
        if j < 2:
            return hT[:, MAIN[j], kc, :]
        return hT[:, 0:2, kc, :]

    hT_main_rngs = [[sub(hT, 2 * 4096, 6 * 4096)], [sub(hT, 6 * 4096, 10 * 4096)], [sub(hT, 0, 2 * 4096)]]

    def v4(bk, n=4):
        return bk.ap.rearrange("p (a b) -> p a b", a=n) if n == 4 else bk[:, 0:n * 128].rearrange(
            "p (a b) -> p a b", a=n)

    issue_upto(T_k)
    for cc in range(2):
        bks = [nbank(), nbank(), nbank()]
        fm_mm(T_k, cc, 16, hT_rhs, hT_main_rngs, bks, [512, 512, 256])
        for j in range(2):
            S.op("dve", lambda e, j=j, cc=cc, bk=bks[j]: e.tensor_copy(out=kT[:, MAIN[j], cc, :], in_=v4(bk)),
                 reads=[bk.rng for bk in [bks[j]]], writes=[sub(kT, (2 + 4 * j) * 512, (6 + 4 * j) * 512)])
        S.op("dve", lambda e, cc=cc, bk=bks[2]: e.tensor_copy(out=kT[:, 0:2, cc, :], in_=v4(bk, 2)),
             reads=[bks[2].rng], writes=[sub(kT, 0, 1024)])
    issue_upto(T_v)
    wv = tiles[T_v][1]
    for p in range(NPOS):
        bk = nbank()
        for kc in range(16):
            S.op("pe", lambda e, kc=kc, p=p, bk=bk: e.matmul(bk[:, 0:256], lhsT=hT[:, p, kc, :], rhs=wv[:, kc, :],
                                                             start=(kc == 0), stop=(kc == 15)),
                 reads=[wv.rng, hT_rng(p)], writes=[sub(bk, 0, 1024)])
        if p % 2 == 0:
            S.op("act", lambda e, p=p, bk=bk: e.activation(out=Vt[:, p, :], in_=bk[:, 0:256], func=AF.Copy),
                 reads=[sub(bk, 0, 1024)], writes=[sub(Vt, p * 512, (p + 1) * 512)])
        else:
            S.op("dve", lambda e, p=p, bk=bk: e.tensor_copy(out=Vt[:, p, :], in_=bk[:, 0:256]),
                 reads=[sub(bk, 0, 1024)], writes=[sub(Vt, p * 512, (p + 1) * 512)])
    for gi_ in range(8):
        issue_upto(T_gate[gi_])
        wg_ = tiles[T_gate[gi_]][1]
        bk = nbank()
        for kc in range(16):
            S.op("pe", lambda e, kc=kc, bk=bk, wg_=wg_: e.matmul(bk[:, 0:256], lhsT=scB_r[:, kc, :], rhs=wg_[:, kc, :],
                                                                  start=(kc == 0), stop=(kc == 15)),
                 reads=[wg_.rng, scB.rng], writes=[bk.rng])
        c0_ = 2 * D + 256 * gi_
        S.op("dve", lambda e, bk=bk, c0_=c0_: e.tensor_tensor(out=modB[:, c0_:c0_ + 256], in0=bk[:, 0:256],
                                                              in1=modB[:, c0_:c0_ + 256], op=ALU.add),
             reads=[bk.rng, sub(modB, c0_ * 4, (c0_ + 256) * 4)], writes=[sub(modB, c0_ * 4, (c0_ + 256) * 4)])
    S.op("dve", lambda e: e.tensor_tensor(out=modGP.ap, in0=modGP.ap, in1=postgB.ap, op=ALU.mult),
         reads=[modGP.rng, postgB.rng], writes=[modGP.rng])
    DMA("sp", biasm.rng, biasm.ap, biasT.rearrange("p (a b c) -> p a b c", a=5, b=8))
    for qi in range(4):
        issue_upto(T_q[qi])
        for cc in range(2):
            hd = 2 * qi + cc
            bks = [nbank(), nbank()]
            fm_mm(T_q[qi], cc, 16, hT_rhs, hT_main_rngs, bks, [512, 512])
            for j in range(2):
                S.op("dve", lambda e, j=j, hd=hd, bk=bks[j]: e.tensor_copy(out=qT[:, 4 * j:4 * j + 4, hd, :],
                                                                           in_=v4(bk)),
                     reads=[bks[j].rng], writes=[sub(qT, 4 * j * 2048, (4 * j + 4) * 2048)])
    for gi in range(4):
        issue_upto(T_ga[gi])
        for cc in range(2):
            hd = 2 * gi + cc
            bks = [nbank(), nbank()]
            fm_mm(T_ga[gi], cc, 16, hT_rhs, hT_main_rngs, bks, [512, 512])
            for j in range(2):
                S.op("act", lambda e, j=j, hd=hd, bk=bks[j]: e.activation(out=sga[:, 4 * j:4 * j + 4, hd, :],
                                                                          in_=v4(bk), func=AF.Silu),
                     reads=[bks[j].rng], writes=[sub(sga, 4 * j * 2048, (4 * j + 4) * 2048)])
    dbg_dump("qT", qT, qT.ap.rearrange("p a b c -> p (a b c)"))
    dbg_dump("kT", kT, kT.ap.rearrange("p a b c -> p (a b c)"))
    dbg_dump("Vt", Vt, Vt.ap.rearrange("p a b -> p (a b)"))
    dbg_dump("sga", sga, sga.ap.rearrange("p a b c -> p (a b c)"))

    if stop == "pb1":
        return finish_early()
    for hh in range(8):
        S.op("dve", lambda e, hh=hh: e.tensor_copy(out=sk_hi[0:1, hh * 128:(hh + 1) * 128],
                                                   in_=expsink[0:1, hh:hh + 1].to_broadcast([1, 128])),
             reads=[expsink.rng], writes=[sub(sk_hi, hh * 256, (hh + 1) * 256)])
    S.op("dve", lambda e: e.tensor_copy(out=sk_f[0:1, :], in_=sk_hi[0:1, :]), reads=[sk_hi.rng], writes=[sk_f.rng])
    for hh in range(8):
        S.op("dve", lambda e, hh=hh: e.tensor_scalar(out=sk_lo[0:1, hh * 128:(hh + 1) * 128],
                                                     in0=sk_f[0:1, hh * 128:(hh + 1) * 128],
                                                     scalar1=-1.0, scalar2=expsink[0:1, hh:hh + 1],
                                                     op0=ALU.mult, op1=ALU.add),
             reads=[sk_f.rng, expsink.rng], writes=[sub(sk_lo, hh * 256, (hh + 1) * 256)])

    its = [(i, kvh) for i in range(8) for kvh in range(2)]

    def pc_geo(it):
        i, kvh = its[it]
        kp = [(i + 1) if i > 0 else 0, i + 2, (i + 3) if i < 7 else 1]
        bi = [3 if i == 0 else 0, 1, 4 if i == 7 else 2]
        sbk = [banks[c_] for c_ in range(3)]
        hs = slice(kvh * 4, kvh * 4 + 4)
        return i, kvh, kp, bi, sbk, hs, PT[it % 2]

    def pc_qk(it):
        i, kvh, kp, bi, sbk, hs, pts = pc_geo(it)
        for c_ in range(3):
            S.op("pe", lambda e, c_=c_, kvh=kvh, i=i, kp=kp, sbk=sbk, hs=hs:
                 e.matmul(sbk[c_].ap, lhsT=kT[:, kp[c_], kvh, :], rhs=qT[:, i, hs, :], start=True, stop=True),
                 reads=[sub(kT, kp[c_] * 512, (kp[c_] + 1) * 512), sub(qT, i * 2048, (i + 1) * 2048)],
                 writes=[sbk[c_].rng])

    def pc_sm(it):
        i, kvh, kp, bi, sbk, hs, pts = pc_geo(it)
        for c_ in range(3):
            S.op("dve", lambda e, c_=c_, sbk=sbk, bi=bi, hs=hs:
                 e.scalar_tensor_tensor(out=scb[c_].ap, in0=v4(sbk[c_]), scalar=QS, in1=biasm[:, bi[c_], hs, :],
                                        op0=ALU.mult, op1=ALU.add),
                 reads=[sbk[c_].rng, biasm.rng], writes=[scb[c_].rng])
            S.op("act", lambda e, c_=c_, pts=pts: e.activation(out=pts[c_].ap, in_=scb[c_].ap, func=AF.Exp),
                 reads=[scb[c_].rng], writes=[pts[c_].rng])

    def pc_pv(it):
        i, kvh, kp, bi, sbk, hs, pts = pc_geo(it)
        obk, dbk = banks[3 + it % 2], banks[5 + it % 2]
        for c_ in range(3):
            S.op("pe", lambda e, c_=c_, kvh=kvh, kp=kp, pts=pts, obk=obk:
                 e.matmul(obk.ap, lhsT=Vt[:, kp[c_], kvh * 128:(kvh + 1) * 128],
                          rhs=pts[c_].ap.rearrange("p a b -> p (a b)"), start=(c_ == 0), stop=(c_ == 2)),
                 reads=[sub(Vt, kp[c_] * 512, (kp[c_] + 1) * 512), pts[c_].rng], writes=[obk.rng])
        for c_ in range(3):
            S.op("pe", lambda e, c_=c_, pts=pts, dbk=dbk:
                 e.matmul(dbk.ap, lhsT=onesb.ap, rhs=pts[c_].ap.rearrange("p a b -> p (a b)"),
                          start=(c_ == 0), stop=False),
                 reads=[onesb.rng, pts[c_].rng], writes=[dbk.rng])
        for j_, skr in enumerate((sk_hi, sk_lo)):
            S.op("pe", lambda e, skr=skr, kvh=kvh, dbk=dbk, j_=j_:
                 e.matmul(dbk.ap, lhsT=onesb[0:1, :], rhs=skr[0:1, kvh * 512:(kvh + 1) * 512],
                          start=False, stop=(j_ == 1)),
                 reads=[onesb.rng, skr.rng], writes=[dbk.rng])

    def pc_fin(it):
        i, kvh, kp, bi, sbk, hs, pts = pc_geo(it)
        obk, dbk = banks[3 + it % 2], banks[5 + it % 2]
        dn = den2[it % 2]
        S.op("act", lambda e, dbk=dbk, dn=dn: e.activation(out=dn.ap, in_=dbk.ap, func=AF.Ln), reads=[dbk.rng],
             writes=[dn.rng])
        S.op("act", lambda e, dn=dn: e.activation(out=dn.ap, in_=dn.ap, func=AF.Exp, scale=-1.0), reads=[dn.rng],
             writes=[dn.rng])
        yrng = sub(yaT, i * 2048 + kvh * 1024, i * 2048 + (kvh + 1) * 1024)
        S.op("dve", lambda e, obk=obk, dn=dn, i=i, hs=hs:
             e.tensor_tensor(out=yaT[:, i, hs, :], in0=v4(obk), in1=dn.ap.rearrange("p (a b) -> p a b", a=4),
                             op=ALU.mult),
             reads=[obk.rng, dn.rng], writes=[yrng])
        S.op("pool", lambda e, i=i, hs=hs: e.tensor_tensor(out=yaT[:, i, hs, :], in0=yaT[:, i, hs, :],
                                                           in1=sga[:, i, hs, :], op=ALU.mult),
             reads=[yrng, sub(sga, i * 2048, (i + 1) * 2048)], writes=[yrng])

    pc_qk(0)
    pc_sm(0)
    for it in range(16):
        if it + 1 < 16:
            pc_qk(it + 1)
        pc_pv(it)
        if it + 1 < 16:
            pc_sm(it + 1)
        pc_fin(it)
    dbg_dump("yaT", yaT, yaT.ap.rearrange("p a b c -> p (a b c)"))

    if stop == "pc":
        return finish_early()
    def halo16(j, kc):
        if j < 2:
            return hT[:, MAIN[j], kc, :]
        st_ = kc * 128 + 120
        return hT_flat[:, st_: st_ + 2 * 1928].rearrange("p (a b) -> p a b", a=2)[:, :, 0:8]

    for g in range(4):
        w_ = 2 ** (g + 1)
        issue_upto(T_u[g])
        for cc in range(2):
            bks = [nbank(), nbank(), nbank()]
            fm_mm(T_u[g], cc, 16, halo16, hT_main_rngs, bks, [512, 512, 16])
            u = ub[cc]
            S.op("dve", lambda e, u=u, bk=bks[0]: e.tensor_copy(out=u[:, 8:520], in_=bk.ap),
                 reads=[bks[0].rng], writes=[u.rng])
            S.op("act", lambda e, u=u, bk=bks[1]: e.activation(out=u[:, 520:1032], in_=bk.ap, func=AF.Copy),
                 reads=[bks[1].rng], writes=[u.rng])
            S.op("dve", lambda e, u=u, bk=bks[2]: e.tensor_scalar(out=u[:, 0:8], in0=bk[:, 0:8],
                                                                   scalar1=pflag[:, 0:1], scalar2=None, op0=ALU.mult),
                 reads=[sub(bks[2], 0, 64), pflag.rng], writes=[u.rng])
            S.op("dve", lambda e, u=u, bk=bks[2]: e.tensor_scalar(out=u[:, 1032:1040], in0=bk[:, 8:16],
                                                                   scalar1=pflag[:, 1:2], scalar2=None, op0=ALU.mult),
                 reads=[sub(bks[2], 0, 64), pflag.rng], writes=[u.rng])
            src = u
            chain = [(1, 1040, 0, 1039, 1, 1040), (2, 1039, 1, 1038, 3, 1040), (4, 1037, 2, 1035, 6, 1039),
                     (8, 1033, 4, 1029, 12, 1037)]
            for lvl in range(g + 1):
                dst = pa if lvl % 2 == 0 else pb
                o0, o1, a0, a1, b0, b1 = chain[lvl]
                S.op("dve", lambda e, dst=dst, src=src, o0=o0, o1=o1, a0=a0, a1=a1, b0=b0, b1=b1:
                     e.tensor_tensor(out=dst[:, o0:o1], in0=src[:, a0:a1], in1=src[:, b0:b1], op=ALU.add),
                     reads=[src.rng], writes=[dst.rng])
                src = dst
            Wb = src
            S.op("dve", lambda e, Wb=Wb, u=u, cc=cc, w_=w_:
                 e.scalar_tensor_tensor(out=pooled[:, cc, :], in0=Wb[:, 8:1032], scalar=1.0 / w_, in1=u[:, 8:1032],
                                        op0=ALU.mult, op1=ALU.subtract),
                 reads=[Wb.rng, u.rng], writes=[sub(pooled, cc * 2048, (cc + 1) * 2048)])
            for (e0, c0_, p0_) in ((8, 0, 0), (1024, 8, 1016)):
                S.op("pool", lambda e, Wb=Wb, e0=e0, c0_=c0_, g=g:
                     e.tensor_tensor(out=etmp[:, 0:8], in0=Wb[:, e0:e0 + 8], in1=cinv[:, g, c0_:c0_ + 8], op=ALU.mult),
                     reads=[Wb.rng, cinv.rng], writes=[etmp.rng])
                S.op("pool", lambda e, u=u, e0=e0, p0_=p0_, cc=cc:
                     e.tensor_tensor(out=pooled[:, cc, p0_:p0_ + 8], in0=etmp[:, 0:8], in1=u[:, e0:e0 + 8],
                                     op=ALU.subtract),
                     reads=[etmp.rng, u.rng], writes=[sub(pooled, cc * 2048, (cc + 1) * 2048)])
        issue_upto(T_gp[g])
        for cc in range(2):
            bks = [nbank(), nbank()]
            fm_mm(T_gp[g], cc, 16, hT_rhs, hT_main_rngs, bks, [512, 512])
            for j in range(2):
                S.op("act", lambda e, j=j, cc=cc, bk=bks[j]: e.activation(out=sgp[:, cc, j * 512:(j + 1) * 512],
                                                                          in_=bk.ap, func=AF.Silu),
                     reads=[bks[j].rng], writes=[sub(sgp, cc * 2048, (cc + 1) * 2048)])
        for dc in range(2):
            for tt in range(2):
                bk = nbank()
                for cch in range(2):
                    S.op("pe", lambda e, g=g, dc=dc, tt=tt, cch=cch, bk=bk:
                         e.matmul(bk.ap, lhsT=wpool[:, g, cch, dc * 128:(dc + 1) * 128],
                                  rhs=pooled[:, cch, tt * 512:(tt + 1) * 512], start=(cch == 0), stop=(cch == 1)),
                         reads=[wpool.rng, sub(pooled, cch * 2048, (cch + 1) * 2048)], writes=[bk.rng])
                kcx = 2 * g + dc
                S.op("dve", lambda e, kcx=kcx, dc=dc, tt=tt, bk=bk:
                     e.scalar_tensor_tensor(out=ypT[:, 4 * tt:4 * tt + 4, kcx, :], in0=v4(bk),
                                            scalar=pscT[:, kcx:kcx + 1],
                                            in1=sgp[:, dc, tt * 512:(tt + 1) * 512].rearrange("p (a b) -> p a b", a=4),
                                            op0=ALU.mult, op1=ALU.mult),
                     reads=[bk.rng, pscT.rng, sub(sgp, dc * 2048, (dc + 1) * 2048)],
                     writes=[sub(ypT, 4 * tt * 2048, (4 * tt + 4) * 2048)])
    dbg_dump("ypT", ypT, ypT.ap.rearrange("p a b c -> p (a b c)"))

    if stop == "pd":
        return finish_early()
    def ya_rhs(j, kc):
        return yaT[:, 4 * j:4 * j + 4, kc, :]

    def yp_rhs(j, kc):
        return ypT[:, 4 * j:4 * j + 4, kc, :]

    ya_rngs = [[sub(yaT, 0, 8192)], [sub(yaT, 8192, 16384)]]
    yp_rngs = [[sub(ypT, 0, 8192)], [sub(ypT, 8192, 16384)]]
    for st in range(8):
        ta, tp, tma, tmp_ = T_e[st]
        issue_upto(tmp_, ta)
        for occ in range(2):
            oc = 2 * st + occ
            fm_mm(tma, occ, 16, hT_rhs, hT_main_rngs, [banks[0], banks[1]], [512, 512])
            fm_mm(tmp_, occ, 16, hT_rhs, hT_main_rngs, [banks[2], banks[3]], [512, 512])
            fm_mm(ta, occ, 8, ya_rhs, ya_rngs, [banks[4], banks[5]], [512, 512])
            fm_mm(tp, occ, 8, yp_rhs, yp_rngs, [banks[6], banks[7]], [512, 512])
            for tt in range(2):
                S.op("act", lambda e, tt=tt, oc=oc: e.activation(out=siga[:, tt * 512:(tt + 1) * 512],
                                                                 in_=banks[tt].ap, func=AF.Sigmoid,
                                                                 bias=bmT[:, oc:oc + 1], scale=1.0),
                     reads=[banks[tt].rng, bmT.rng], writes=[sub(siga, tt * 2048, (tt + 1) * 2048)])
            for tt in range(2):
                S.op("act", lambda e, tt=tt, oc=oc: e.activation(out=sigp[:, tt * 512:(tt + 1) * 512],
                                                                 in_=banks[2 + tt].ap, func=AF.Sigmoid,
                                                                 bias=bmT[:, 16 + oc:17 + oc], scale=1.0),
                     reads=[banks[2 + tt].rng, bmT.rng], writes=[sub(sigp, tt * 2048, (tt + 1) * 2048)])
            for tt in range(2):
                S.op("dve", lambda e, tt=tt: e.tensor_tensor(out=t1[:, tt * 512:(tt + 1) * 512],
                                                             in0=banks[4 + tt].ap,
                                                             in1=siga[:, tt * 512:(tt + 1) * 512], op=ALU.mult),
                     reads=[banks[4 + tt].rng, sub(siga, tt * 2048, (tt + 1) * 2048)],
                     writes=[sub(t1, tt * 2048, (tt + 1) * 2048)])
                S.op("dve", lambda e, tt=tt: e.tensor_tensor(out=t2[:, tt * 512:(tt + 1) * 512],
                                                             in0=banks[6 + tt].ap,
                                                             in1=sigp[:, tt * 512:(tt + 1) * 512], op=ALU.mult),
                     reads=[banks[6 + tt].rng, sub(sigp, tt * 2048, (tt + 1) * 2048)],
                     writes=[sub(t2, tt * 2048, (tt + 1) * 2048)])
                S.op("pool", lambda e, tt=tt, oc=oc:
                     e.tensor_tensor(out=mergedT[:, 4 * tt:4 * tt + 4, oc, :],
                                     in0=t1[:, tt * 512:(tt + 1) * 512].rearrange("p (a b) -> p a b", a=4),
                                     in1=t2[:, tt * 512:(tt + 1) * 512].rearrange("p (a b) -> p a b", a=4), op=ALU.add),
                     reads=[sub(t1, tt * 2048, (tt + 1) * 2048), sub(t2, tt * 2048, (tt + 1) * 2048)],
                     writes=[sub(mergedT, 4 * tt * 4096, (4 * tt + 4) * 4096)])
    dbg_dump("mergedT", mergedT, mergedT.ap.rearrange("p a b c -> p (a b c)"))

    if stop == "pe":
        return finish_early()
    outs = []

    def pf_final(i):
        ob = obuf[i]
        x = xf[i % 2]
        DMA("sp", x.rng, x.ap, xh[128 + 128 * i: 256 + 128 * i, :], name="xf")
        S.op("dve", lambda e, i=i: e.tensor_reduce(out=sst[:, i:i + 1], in_=ssf[:, 4 * i:4 * i + 4], axis=AX.X,
                                                   op=ALU.add),
             reads=[sub(ssf, 16 * i, 16 * i + 16)], writes=[sub(sst, 4 * i, 4 * i + 4)])
        S.op("act", lambda e, i=i: e.activation(out=rstdf[:, i:i + 1], in_=sst[:, i:i + 1], func=AF.Sqrt,
                                                bias=epst[:, 0:1], scale=1.0 / D),
             reads=[sub(sst, 4 * i, 4 * i + 4), epst.rng], writes=[sub(rstdf, 4 * i, 4 * i + 4)])
        S.op("dve", lambda e, i=i: e.reciprocal(out=rstdf[:, i:i + 1], in_=rstdf[:, i:i + 1]),
             reads=[sub(rstdf, 4 * i, 4 * i + 4)], writes=[sub(rstdf, 4 * i, 4 * i + 4)])
        S.op("dve", lambda e, i=i, ob=ob: e.scalar_tensor_tensor(out=ob.ap, in0=ob.ap, scalar=rstdf[:, i:i + 1],
                                                                 in1=modGP.ap, op0=ALU.mult, op1=ALU.mult),
             reads=[ob.rng, sub(rstdf, 4 * i, 4 * i + 4), modGP.rng], writes=[ob.rng])
        S.op("pool" if i % 2 == 0 else "dve",
             lambda e, ob=ob, x=x: e.tensor_tensor(out=ob.ap, in0=ob.ap, in1=x.ap, op=ALU.add),
             reads=[ob.rng, x.rng], writes=[ob.rng])
        outs.append(DMA("sp", None, out_d[128 * i:128 * (i + 1), :], ob.ap, reads=[ob.rng], name="out"))

    issue_upto(T_o[3], T_o[0])
    wos = [tiles[T_o[n]][1] for n in range(4)]

    def pf_evac(i, n, bk):
        ob = obuf[i]
        S.op("dve", lambda e, ob=ob, n=n, bk=bk: e.tensor_copy(out=ob[:, n * 512:(n + 1) * 512], in_=bk.ap),
             reads=[bk.rng], writes=[sub(ob, n * 2048, (n + 1) * 2048)])
        junkF = junkF2[n % 2]
        S.op("dve", lambda e, i=i, n=n, ob=ob, junkF=junkF:
             e.scalar_tensor_tensor(out=junkF.ap, in0=ob[:, n * 512:(n + 1) * 512], scalar=1.0,
                                    in1=ob[:, n * 512:(n + 1) * 512], op0=ALU.mult, op1=ALU.mult,
                                    accum_out=ssf[:, 4 * i + n:4 * i + n + 1]),
             reads=[sub(ob, n * 2048, (n + 1) * 2048)],
             writes=[junkF.rng, sub(ssf, 16 * i + 4 * n, 16 * i + 4 * n + 4)])

    for i in range(8):
        bk = nbank()
        for kc in range(16):
            S.op("pe", lambda e, kc=kc, i=i, bk=bk: e.matmul(bk.ap, lhsT=mergedT[:, i, kc, :], rhs=wos[0][:, kc, :],
                                                             start=(kc == 0), stop=(kc == 15)),
                 reads=[wos[0].rng, sub(mergedT, i * 4096, (i + 1) * 4096)], writes=[bk.rng])
        pf_evac(i, 0, bk)
    for i in range(8):
        bks = [nbank() for _ in range(3)]
        for kc in range(16):
            for n in range(1, 4):
                S.op("pe", lambda e, kc=kc, i=i, n=n, bks=bks: e.matmul(bks[n - 1].ap, lhsT=mergedT[:, i, kc, :],
                                                                         rhs=wos[n][:, kc, :], start=(kc == 0),
                                                                         stop=(kc == 15)),
                     reads=[wos[n].rng, sub(mergedT, i * 4096, (i + 1) * 4096)], writes=[bks[n - 1].rng])
        for n in range(1, 4):
            pf_evac(i, n, bks[n - 1])
        if i >= 1:
            pf_final(i - 1)
    pf_final(7)

    S.emit(nc, es)
    es.close()
    return nc


def _t5_bucket(rel):
    half, max_exact = 16, 8
    ret = np.where(rel > 0, half, 0)
    n = np.abs(rel)
    nf = np.maximum(n, 1).astype(np.float32)
    large = max_exact + (np.log(nf / max_exact) / math.log(128 / max_exact) * (half - max_exact)).astype(np.int32)
    large = np.minimum(large, half - 1)
    return ret + np.where(n < max_exact, n, large)


def _host_inputs(x, c, rel_bias_table, w_ada, b_ada, pre_norm_g, post_norm_g, w_in, attn_sink, w_pool_group,
                 pool_scale, w_branch_attn, w_branch_pool, w_merge, b_merge, w_out):
    f = lambda a: np.ascontiguousarray(np.asarray(a, dtype=np.float32))
    x2 = f(x)[0]
    xpad = np.zeros((S_TOT + 256, D), np.float32)
    xpad[128:128 + S_TOT] = x2
    key = np.arange(128)[:, None]
    qry = np.arange(128)[None, :]
    tab = f(rel_bias_table)
    tiles = []
    for cidx in (-1, 0, 1):
        rel = key + 128 * cidx - qry
        g = tab[_t5_bucket(rel)]
        g = np.transpose(g, (0, 2, 1))
        valid = (np.abs(rel) <= 128)[:, None, :]
        tiles.append(np.where(valid, g, np.float32(NEG)).astype(np.float32))
    allneg = np.full_like(tiles[0], NEG)
    shared = dict(
        cT=f(np.asarray(c)[0].reshape(16, 128).T), w_ada=f(w_ada)[0], b_ada=f(b_ada)[0][None, :],
        pre_g=f(pre_norm_g)[0][None, :], post_g=f(post_norm_g)[0][None, :], w_in=f(w_in)[0],
        w_pool=f(w_pool_group)[0], w_a=f(w_branch_attn)[0], w_p=f(w_branch_pool)[0], w_m=f(w_merge)[0],
        w_out=f(w_out)[0], bmT=f(f(b_merge)[0].reshape(32, 128).T), pscT=f(f(pool_scale)[0].reshape(8, 128).T),
        sinkB=f(np.broadcast_to(f(attn_sink)[0][None, :], (128, 8))), identf=np.eye(128, dtype=np.float32))
    maps = []
    for k in range(NCORES):
        m = dict(shared)
        m["xh"] = np.ascontiguousarray(xpad[k * T: k * T + T + 256])
        first = allneg if k == 0 else tiles[0]
        last = allneg if k == NCORES - 1 else tiles[2]
        m["biasT"] = np.ascontiguousarray(np.stack([tiles[0], tiles[1], tiles[2], first, last], axis=1)
                                          .reshape(128, 5 * 8 * 128))
        m["pflag"] = np.ascontiguousarray(np.broadcast_to(
            np.array([0.0 if k == 0 else 1.0, 0.0 if k == NCORES - 1 else 1.0], np.float32)[None, :], (128, 2)))
        ci = np.zeros((4, 16), np.float32)
        for g in range(4):
            w = 2 ** (g + 1)
            for j in range(16):
                tl = j if j < 8 else T - 16 + j
                gi = k * T + tl
                lo, hi = max(gi - w // 2, 0), min(gi + w // 2, S_TOT)
                ci[g, j] = 1.0 / float(hi - lo)
        m["cinv"] = np.ascontiguousarray(np.broadcast_to(ci.reshape(1, 64), (128, 64)))
        maps.append(m)
    return maps


_NC_CACHE = {}


def kernel(**inputs):
    maps = _host_inputs(**inputs)
    if "nc" not in _NC_CACHE:
        _NC_CACHE["nc"] = build_program()
    nc = _NC_CACHE["nc"]
    res = run_bass_kernel_spmd(nc, maps, core_ids=list(range(NCORES)))
    out = np.concatenate([np.asarray(r["out"], dtype=np.float32) for r in res.results], axis=0)
    return out.reshape(1, S_TOT, D)
```

```python
import math
import os as _os
from contextlib import ExitStack

import numpy as np
import concourse.bass as bass
import concourse.mybir as mybir
from concourse.bass_utils import run_bass_kernel_spmd

F32, BF16, F32R = mybir.dt.float32, mybir.dt.bfloat16, mybir.dt.float32r
AF = mybir.ActivationFunctionType
ALU = mybir.AluOpType
AX = mybir.AxisListType

NCORES = 8
S_TOT, D = 8192, 2048
T = S_TOT // NCORES
NPOS = 10
EPS = 1e-6
NEG = -30000.0
IN_W = 4608
QS = 128 ** -0.5
ENGS = ("sp", "act", "pool", "dve", "pe")
NRING = 16


class Op:
    __slots__ = ("eng", "fn", "deps", "dma", "sem", "val", "users", "name")

    def __init__(self, eng, fn, dma, name):
        self.eng, self.fn, self.dma, self.name = eng, fn, dma, name
        self.deps = set()
        self.sem = None
        self.val = 0
        self.users = 0


class Rec:
    __slots__ = ("lo", "hi", "w", "r")

    def __init__(self, lo, hi, w):
        self.lo, self.hi, self.w, self.r = lo, hi, w, {}


class Sched:
    def __init__(self):
        self.q = {e: [] for e in ENGS}
        self.recs = {"sb": [], "ps": []}
        self.ndma = {e: [] for e in ENGS}

    @staticmethod
    def _ordered(d, o):
        return (not d.dma) and (not o.dma) and d.eng == o.eng and d.eng == "pe"

    def op(self, eng, fn, reads=(), writes=(), dma=False, name=""):
        o = Op(eng, fn, dma, name)
        deps = set()
        reads = [(sp, lo // 2048 * 2048, -(-hi // 2048) * 2048) if sp == "ps" else (sp, lo, hi)
                 for (sp, lo, hi) in reads]
        writes = [(sp, lo // 2048 * 2048, -(-hi // 2048) * 2048) if sp == "ps" else (sp, lo, hi)
                  for (sp, lo, hi) in writes]
        for (sp, lo, hi) in reads:
            for r in self.recs[sp]:
                if r.lo < hi and lo < r.hi:
                    if r.w is not None and r.w is not o:
                        d = r.w
                        if not (d.eng == "pe" and eng == "pe" and not dma):
                            deps.add(d)
                    if sp == "ps":
                        for x in r.r.values():
                            if x is not o and x.eng != eng:
                                deps.add(x)
                    r.r[id(o) if dma else eng] = o
        for (sp, lo, hi) in writes:
            new = []
            for r in self.recs[sp]:
                if r.lo < hi and lo < r.hi:
                    if r.w is not None and r.w is not o and not self._ordered(r.w, o):
                        deps.add(r.w)
                    for x in r.r.values():
                        if x is not o and not self._ordered(x, o):
                            deps.add(x)
                    if lo <= r.lo and r.hi <= hi:
                        continue
                new.append(r)
            new.append(Rec(lo, hi, o))
            self.recs[sp] = new
        if dma:
            ring = self.ndma[eng]
            if len(ring) >= NRING:
                deps.add(ring[len(ring) - NRING])
            ring.append(o)
        deps.discard(o)
        o.deps = deps
        for d in deps:
            d.users += 1
        self.q[eng].append(o)
        return o

    def emit(self, nc, es):
        engsem = {e: es.enter_context(nc.semaphore("s_" + e)) for e in ENGS}
        rings = {e: [es.enter_context(nc.semaphore("d_%s%d" % (e, i))) for i in range(NRING)]
                 for e in ENGS if self.ndma[e]}
        for e in ENGS:
            cnt = 0
            for o in self.q[e]:
                if o.dma:
                    continue
                if o.users > 0:
                    cnt += 1
                    o.sem, o.val = engsem[e], cnt
        for e in ENGS:
            for i, o in enumerate(self.ndma[e]):
                o.sem, o.val = rings[e][i % NRING], 16 * (i // NRING + 1)

        def run(e, eng):
            waited = {}
            for o in self.q[e]:
                need = {}
                for d in o.deps:
                    k = id(d.sem)
                    if k not in need or need[k][1] < d.val:
                        need[k] = (d.sem, d.val)
                for k, (sem, val) in need.items():
                    if waited.get(k, 0) >= val:
                        continue
                    waited[k] = val
                    eng.wait_ge(sem, val)
                ins = o.fn(eng)
                if o.dma:
                    ins.then_inc(o.sem, 16)
                elif o.users > 0:
                    ins.then_inc(o.sem, 1)
            for i, sem in enumerate(rings.get(e, [])):
                n = len(self.ndma[e])
                uses = (n - i + NRING - 1) // NRING if n > i else 0
                if uses > 0:
                    eng.wait_ge(sem, 16 * uses)

        block = es.enter_context(nc.Block())
        block_engines = {"sp": block.sync, "act": block.scalar, "pool": block.gpsimd, "dve": block.vector,
                         "pe": block.tensor}
        for e in ENGS:
            block_engines[e](lambda eng, e=e: run(e, eng))


class Buf:
    def __init__(self, space, ap, lo, hi):
        self.space, self.ap, self.lo, self.hi = space, ap, lo, hi

    @property
    def rng(self):
        return (self.space, self.lo, self.hi)

    def __getitem__(self, k):
        return self.ap[k]


def build_program(debug=(), stop=None):
    nc = bass.Bass("TRN2", target_bir_lowering=False)

    def din(name, shape):
        return nc.dram_tensor(name, list(shape), F32, kind="ExternalInput").ap()

    xh = din("xh", [T + 256, D])
    cT_d = din("cT", [128, 16])
    w_ada = din("w_ada", [D, 3 * D])
    b_ada = din("b_ada", [1, 3 * D])
    pre_g = din("pre_g", [1, D])
    post_g = din("post_g", [1, D])
    w_in = din("w_in", [D, IN_W])
    w_pool = din("w_pool", [4, 256, 256])
    w_a = din("w_a", [1024, D])
    w_p = din("w_p", [1024, D])
    w_m = din("w_m", [D, 2 * D])
    w_out = din("w_out", [D, D])
    biasT = din("biasT", [128, 5 * 8 * 128])
    bmT_d = din("bmT", [128, 32])
    pscT_d = din("pscT", [128, 8])
    sinkB_d = din("sinkB", [128, 8])
    pflag_d = din("pflag", [128, 2])
    cinv_d = din("cinv", [128, 64])
    ident_d = din("identf", [128, 128])
    out_d = nc.dram_tensor("out", [T, D], F32, kind="ExternalOutput").ap()
    dbg_d = {}
    for (nm, shp, dt_) in debug:
        dbg_d[nm] = nc.dram_tensor("dbg_" + nm, list(shp), dt_, kind="ExternalOutput").ap()

    S = Sched()
    es = ExitStack()
    ARENA_B = 207872
    arena = es.enter_context(nc.sbuf_tensor("arena", [128, ARENA_B // 4], F32))
    psum = es.enter_context(nc.psum_tensor("psum", [128, 4096], F32))

    def sb(off, dtype, shape):
        n = int(np.prod(shape))
        esz = 2 if dtype == BF16 else 4
        assert off % 4 == 0 and (n * esz) % 4 == 0
        ap = arena[:, off // 4: off // 4 + (n * esz) // 4]
        if dtype != F32:
            ap = ap.bitcast(dtype)
        if len(shape) == 2:
            ap = ap.rearrange("p (a b) -> p a b", a=shape[0])
        elif len(shape) == 3:
            ap = ap.rearrange("p (a b c) -> p a b c", a=shape[0], b=shape[1])
        assert off + n * esz <= ARENA_B
        return Buf("sb", ap, off, off + n * esz)

    def sub(buf, lo, hi):
        return (buf.space, buf.lo + lo, buf.lo + hi)

    def bank(k, dtype=F32):
        ap = psum[:, 512 * k: 512 * (k + 1)]
        if dtype != F32:
            ap = ap.bitcast(dtype)
        return Buf("ps", ap, 2048 * k, 2048 * (k + 1))

    banks = [bank(k) for k in range(8)]

    A0, B0, W0, M0, X0, C0 = 0, 40960, 61440, 110592, 135168, 194560
    hT = sb(A0, BF16, (NPOS, 16, 128))
    hT_flat = sb(A0, BF16, (NPOS * 16 * 128,))

    def hT_rng(p, half=None):
        if half is None:
            return sub(hT, p * 4096, (p + 1) * 4096)
        return sub(hT, p * 4096 + half * 2048, p * 4096 + (half + 1) * 2048)

    biasm = sb(B0, F32, (5, 8, 128))
    postgB = sb(B0, F32, (D,))
    modS = sb(M0, F32, (D,))
    modG = sb(M0 + 8192, F32, (D,))
    modGP = sb(M0 + 16384, F32, (D,))
    modB = sb(M0, F32, (3 * D,))
    yaT = sb(M0, BF16, (8, 8, 128))
    sk_f = sb(M0, F32, (1024,))
    scB = sb(X0, BF16, (16, 128))
    xt = [sb(X0 + 8192, F32, (D,)), sb(X0 + 16384, F32, (D,)), sb(B0 + 8192, F32, (D,)), sb(X0 + 45056, F32, (D,))]
    tbuf = sb(X0 + 24576, F32, (D,))
    hb = [sb(X0 + 32768, BF16, (D,)), sb(X0 + 36864, BF16, (D,))]
    junkA2 = [sb(X0 + 40960, BF16, (D,)), sb(X0 + 53248, BF16, (D,))]
    pregB = sb(X0 + 45056, F32, (D,))
    ada_slot = [sb(W0 + 8192 * i, BF16, (16, 256)) for i in range(6)]
    qT = sb(X0, BF16, (8, 8, 128))
    sga = sb(X0 + 16384, BF16, (8, 8, 128))
    kT = sb(X0 + 32768, BF16, (NPOS, 2, 128))
    Vt = sb(X0 + 37888, BF16, (NPOS, 256))
    scb = [sb(X0 + 43008 + 2048 * i, F32, (4, 128)) for i in range(3)]
    PT = [[sb(X0 + 49152 + 3072 * j + 1024 * c, BF16, (4, 128)) for c in range(3)] for j in range(2)]
    den = sb(X0 + 55296, F32, (512,))
    den2 = [den, sb(X0 + 57344, F32, (512,))]
    osb = sb(X0 + 57344, F32, (4, 128))
    ypT = sb(X0, BF16, (8, 8, 128))
    ub = [sb(X0 + 16384, F32, (1040,)), sb(X0 + 20544, F32, (1040,))]
    pa = sb(X0 + 24704, F32, (1040,))
    pb = sb(X0 + 28864, F32, (1040,))
    sgp = sb(X0 + 33024, BF16, (2, 1024))
    pooled = sb(X0 + 37120, BF16, (2, 1024))
    etmp = sb(X0 + 41216, F32, (16,))
    mergedT = sb(X0 + 16384, BF16, (8, 16, 128))
    siga = sb(B0, F32, (1024,))
    sigp = sb(B0 + 4096, F32, (1024,))
    t1 = sb(B0 + 8192, F32, (1024,))
    t2 = sb(B0 + 12288, F32, (1024,))
    obuf = [sb(A0 + 8192 * i, F32, (D,)) for i in range(5)] + \
           [sb(B0, F32, (D,)), sb(B0 + 8192, F32, (D,)), sb(X0 + 49152, F32, (D,))]
    xf = [sb(M0, F32, (D,)), sb(M0 + 8192, F32, (D,))]
    junkF2 = [sb(X0 + 57344, BF16, (512,)), sb(X0 + 58368, BF16, (512,))]
    c = C0
    def calloc(dtype, shape):
        nonlocal c
        b = sb(c, dtype, shape)
        c = b.hi
        c = (c + 31) // 32 * 32
        return b
    ident = calloc(BF16, (128,))
    onesb = calloc(BF16, (128,))
    onesf = calloc(F32, (128,))
    identf = calloc(F32, (128,))
    cT = calloc(F32, (16,))
    scs = calloc(F32, (16,))
    bmT = calloc(F32, (32,))
    pscT = calloc(F32, (8,))
    sinkB = calloc(F32, (8,))
    expsink = calloc(F32, (8,))
    pflag = calloc(F32, (2,))
    cinv = calloc(F32, (4, 16))
    ss = calloc(F32, (16,))
    rstd = calloc(F32, (16,))
    ssf = calloc(F32, (32,))
    sst = calloc(F32, (8,))
    rstdf = calloc(F32, (8,))
    epst = calloc(F32, (2,))
    wpool = calloc(BF16, (4, 2, 256))
    sk_hi = calloc(BF16, (1024,))
    sk_lo = calloc(BF16, (1024,))
    wslot = [(W0 + 8192 * i) for i in range(6)]

    def DMA(eng, out_buf_rng, out_ap, in_ap, reads=(), name="dma"):
        return S.op(eng, lambda e: e.dma_start(out=out_ap, in_=in_ap), reads=list(reads),
                    writes=[out_buf_rng] if out_buf_rng is not None else [], dma=True, name=name)

    def dbg_dump(nm, buf, ap2d):
        if nm in dbg_d:
            DMA("sp", None, dbg_d[nm], ap2d, reads=[buf.rng], name="dbg")

    def finish_early():
        DMA("sp", None, out_d[0:128, :], modB[:, 0:D], reads=[modB.rng], name="out")
        S.emit(nc, es)
        es.close()
        return nc

    DMA("sp", identf.rng, identf.ap, ident_d)
    DMA("sp", cT.rng, cT.ap, cT_d)
    S.op("dve", lambda e: e.memset(onesf.ap, 1.0), writes=[onesf.rng])
    S.op("dve", lambda e: e.memset(onesb.ap, 1.0), writes=[onesb.rng])
    S.op("dve", lambda e: e.memset(ss.ap, 0.0), writes=[ss.rng])
    S.op("dve", lambda e: e.memset(ssf.ap, 0.0), writes=[ssf.rng])
    S.op("dve", lambda e: e.memset(epst.ap, EPS), writes=[epst.rng])
    S.op("dve", lambda e: e.tensor_copy(out=ident.ap, in_=identf.ap), reads=[identf.rng], writes=[ident.rng])
    S.op("act", lambda e: e.activation(out=scs.ap, in_=cT.ap, func=AF.Silu), reads=[cT.rng], writes=[scs.rng])
    scB_r = scB.ap
    for kc in range(16):
        S.op("dve", lambda e, kc=kc: e.tensor_scalar(out=scB_r[:, kc, :], in0=onesf.ap, scalar1=scs[:, kc:kc + 1],
                                                      scalar2=None, op0=ALU.mult),
             reads=[onesf.rng, scs.rng], writes=[sub(scB, kc * 256, (kc + 1) * 256)])
    def xrow(p):
        r = 0 if p == 0 else (9 if p == 1 else p - 1)
        return xh[r * 128:(r + 1) * 128, :]
    for p in range(3):
        DMA("sp", xt[p].rng, xt[p].ap, xrow(p), name="x")
    for j in range(3):
        DMA("sp", sub(modB, j * 8192, (j + 1) * 8192), modB[:, j * D:(j + 1) * D],
            b_ada[0:1, j * D:(j + 1) * D].partition_broadcast(128))
    DMA("sp", pregB.rng, pregB.ap, pre_g[0:1, :].partition_broadcast(128))
    DMA("sp", postgB.rng, postgB.ap, post_g[0:1, :].partition_broadcast(128))
    for (bf, dd) in ((bmT, bmT_d), (pscT, pscT_d), (sinkB, sinkB_d), (pflag, pflag_d)):
        DMA("sp", bf.rng, bf.ap, dd)
    DMA("sp", cinv.rng, cinv.ap, cinv_d.rearrange("p (a b) -> p a b", a=4))
    S.op("act", lambda e: e.activation(out=expsink.ap, in_=sinkB.ap, func=AF.Exp), reads=[sinkB.rng],
         writes=[expsink.rng])

    tiles = []
    state = {"slot": 0, "issued": 0}
    tenant = {}

    def add_tile(src, kc, ncols, off_in_slot=0, newslot=True, nslots=1, fixed=None):
        if fixed is not None:
            base_, prev_ = fixed
            b = sb(base_, BF16, (kc, ncols))
            tiles.append((src, b, prev_))
            return len(tiles) - 1
        if newslot:
            if nslots == 2 and state["slot"] % 2 == 1:
                state["slot"] += 1
            first = state["slot"] % 6
            state["slot"] += nslots
        else:
            first = (state["slot"] - 1) % 6
        base = wslot[first]
        b = sb(base + off_in_slot, BF16, (kc, ncols))
        idx = len(tiles)
        prev = -1
        for sl_ in range(first, first + nslots):
            cur, old = tenant.get(sl_, ([], -1))
            if newslot:
                p_ = max(cur) if cur else -1
                tenant[sl_] = ([idx], p_)
                prev = max(prev, p_)
            else:
                cur.append(idx)
                prev = max(prev, old)
        tiles.append((src, b, prev))
        return idx

    def issue_upto(n_last, n_first=None):
        if n_first is None:
            n_first = n_last
        lim = min(n_last + LA, len(tiles) - 1)
        while state["issued"] <= lim:
            src, b, prev = tiles[state["issued"]]
            if state["issued"] > n_last and prev >= n_first:
                break
            assert prev < n_first
            DMA("pool", b.rng, b.ap, src, name="w%d" % state["issued"])
            state["issued"] += 1
        assert state["issued"] > n_last

    w_in_v = w_in.rearrange("(kc p) n -> p kc n", p=128)
    w_m_v = w_m.rearrange("(kc p) n -> p kc n", p=128)
    w_a_v = w_a.rearrange("(kc p) n -> p kc n", p=128)
    w_p_v = w_p.rearrange("(kc p) n -> p kc n", p=128)
    w_out_v = w_out.rearrange("(kc p) n -> p kc n", p=128)

    def win_tile(c0):
        return add_tile(w_in_v[:, :, c0:c0 + 256], 16, 256)

    T_k = win_tile(1024)
    T_v = win_tile(1280)
    T_q = [win_tile(0), win_tile(256)]
    T_ga = [win_tile(1536 + 256 * i) for i in range(4)]
    T_q += [win_tile(512), win_tile(768)]
    T_u, T_gp = [], []
    for g in range(4):
        T_u.append(win_tile(2560 + 256 * g))
        T_gp.append(win_tile(3584 + 256 * g))
    T_e = []
    for st in range(8):
        c0 = st * 256
        ta = add_tile(w_a_v[:, :, c0:c0 + 256], 8, 256)
        tp = add_tile(w_p_v[:, :, c0:c0 + 256], 8, 256, off_in_slot=4096, newslot=False)
        tma = add_tile(w_m_v[:, :, c0:c0 + 256], 16, 256)
        tmp_ = add_tile(w_m_v[:, :, D + c0:D + c0 + 256], 16, 256)
        T_e.append((ta, tp, tma, tmp_))
    T_o = [add_tile(w_out_v[:, :, n * 512:(n + 1) * 512], 16, 512, nslots=2) for n in range(3)]
    T_o.append(add_tile(w_out_v[:, :, 3 * 512:4 * 512], 16, 512, fixed=(X0, T_e[-1][3])))
    LA = 4


    def issue_cap(k):
        while state["issued"] <= k:
            src, b, prev = tiles[state["issued"]]
            assert prev < 0
            DMA("pool", b.rng, b.ap, src, name="w%d" % state["issued"])
            state["issued"] += 1

    if stop == "setup":
        return finish_early()
    w_ada_v = w_ada.rearrange("(kc p) n -> p kc n", p=128)

    def p0_dma(j):
        sl = ada_slot[j % 6]
        DMA("pool", sl.rng, sl.ap, w_ada_v[:, :, j * 256:(j + 1) * 256], name="wada")

    def p0_consume(j):
        sl = ada_slot[j % 6]
        bk = banks[j % 2]
        for kc in range(16):
            S.op("pe", lambda e, kc=kc: e.matmul(bk[:, 0:256], lhsT=scB_r[:, kc, :], rhs=sl[:, kc, :],
                                                  start=(kc == 0), stop=(kc == 15)),
                 reads=[sl.rng, scB.rng], writes=[bk.rng])
        S.op("dve", lambda e: e.tensor_tensor(out=modB[:, j * 256:(j + 1) * 256], in0=bk[:, 0:256],
                                              in1=modB[:, j * 256:(j + 1) * 256], op=ALU.add),
             reads=[bk.rng, sub(modB, j * 1024, (j + 1) * 1024)],
             writes=[sub(modB, j * 1024, (j + 1) * 1024)])

    for j in range(6):
        p0_dma(j)
    for j in range(16):
        p0_consume(j)
        if j + 6 <= 21:
            p0_dma(j + 6)
    S.op("dve", lambda e: e.scalar_tensor_tensor(out=modG.ap, in0=modG.ap, scalar=1.0, in1=pregB.ap,
                                                 op0=ALU.add, op1=ALU.mult),
         reads=[modG.rng, pregB.rng], writes=[modG.rng])

    if stop == "p0":
        return finish_early()
    def pa_s1(p):
        x = xt[p % 4]
        if p >= 3:
            DMA("sp", x.rng, x.ap, xrow(p), name="x")
        junkA = junkA2[p % 2]
        S.op("act", lambda e: e.activation(out=junkA.ap, in_=x.ap, func=AF.Square, accum_out=ss[:, p:p + 1]),
             reads=[x.rng], writes=[junkA.rng, sub(ss, 4 * p, 4 * p + 4)])
        S.op("act", lambda e: e.activation(out=rstd[:, p:p + 1], in_=ss[:, p:p + 1], func=AF.Sqrt,
                                           bias=epst[:, 0:1], scale=1.0 / D),
             reads=[sub(ss, 4 * p, 4 * p + 4), epst.rng], writes=[sub(rstd, 4 * p, 4 * p + 4)])
        S.op("dve", lambda e: e.reciprocal(out=rstd[:, p:p + 1], in_=rstd[:, p:p + 1]),
             reads=[sub(rstd, 4 * p, 4 * p + 4)], writes=[sub(rstd, 4 * p, 4 * p + 4)])
        S.op("dve", lambda e: e.scalar_tensor_tensor(out=tbuf.ap, in0=x.ap, scalar=rstd[:, p:p + 1], in1=modG.ap,
                                                     op0=ALU.mult, op1=ALU.mult),
             reads=[x.rng, sub(rstd, 4 * p, 4 * p + 4), modG.rng], writes=[tbuf.rng])
        h = hb[p % 2]
        S.op("pool" if p % 2 == 0 else "dve",
             lambda e: e.tensor_tensor(out=h.ap, in0=tbuf.ap, in1=modS.ap, op=ALU.add),
             reads=[tbuf.rng, modS.rng], writes=[h.rng])

    def pa_s2(p):
        h = hb[p % 2]
        k0 = 2 + 2 * (p % 2)
        pT = psum[:, 512 * k0: 512 * (k0 + 2)].bitcast(BF16)
        prng = ("ps", 2048 * k0, 2048 * (k0 + 2))
        for kc in range(16):
            S.op("pe", lambda e, kc=kc: e.transpose(pT[:, kc * 128:(kc + 1) * 128], h[:, kc * 128:(kc + 1) * 128],
                                                    ident.ap),
                 reads=[h.rng, ident.rng], writes=[prng])
        S.op("act", lambda e: e.activation(out=hT[:, p, 0:8, :],
                                           in_=pT[:, 0:1024].rearrange("p (a b) -> p a b", a=8), func=AF.Copy),
             reads=[("ps", 2048 * k0, 2048 * (k0 + 1))], writes=[hT_rng(p, 0)])
        S.op("dve", lambda e: e.tensor_copy(out=hT[:, p, 8:16, :],
                                            in_=pT[:, 1024:2048].rearrange("p (a b) -> p a b", a=8)),
             reads=[("ps", 2048 * (k0 + 1), 2048 * (k0 + 2))], writes=[hT_rng(p, 1)])

    pa_s1(0)
    for p in range(NPOS):
        if p + 1 < NPOS:
            pa_s1(p + 1)
        pa_s2(p)
        if p == 2:
            p0_consume(16)
            p0_consume(17)
        if p == 4:
            p0_consume(18)
            p0_consume(19)
            p0_dma(22)
            p0_dma(23)
        if p == 6:
            p0_consume(20)
            p0_consume(21)
        if p == 8:
            issue_cap(3)
    p0_consume(22)
    p0_consume(23)
    S.op("dve", lambda e: e.tensor_tensor(out=modGP.ap, in0=modGP.ap, in1=postgB.ap, op=ALU.mult),
         reads=[modGP.rng, postgB.rng], writes=[modGP.rng])
    dbg_dump("hT", hT, hT_flat.ap)

    if stop == "pa":
        return finish_early()
    DMA("sp", biasm.rng, biasm.ap, biasT.rearrange("p (a b c) -> p a b c", a=5, b=8))
    DMA("pool", wpool.rng, wpool.ap, w_pool.rearrange("g (cc p) d -> p g cc d", p=128), name="wpool")

    bank_ctr = {"n": 0}

    def nbank():
        b = banks[bank_ctr["n"] % 8]
        bank_ctr["n"] += 1
        return b

    MAIN = (slice(2, 6), slice(6, 10))

    def fm_mm(ti, cc, kcs, rhs_fn, rhs_rngs, bks, widths):
        wb = tiles[ti][1]
        for kc in range(kcs):
            for j, bk in enumerate(bks):
                S.op("pe", lambda e, kc=kc, j=j, bk=bk: e.matmul(bk[:, 0:widths[j]],
                                                                  lhsT=wb[:, kc, cc * 128:(cc + 1) * 128],
                                                                  rhs=rhs_fn(j, kc),
                                                                  start=(kc == 0), stop=(kc == kcs - 1)),
                     reads=[wb.rng] + rhs_rngs[j], writes=[sub(bk, 0, widths[j] * 4)])

    def hT_rhs(j, kc):
        if j < 2:
            return hT[:, MAIN[j], kc, :]
        return hT[:, 0:2, kc, :]

    hT_main_rngs = [[sub(hT, 2 * 4096, 6 * 4096)], [sub(hT, 6 * 4096, 10 * 4096)], [sub(hT, 0, 2 * 4096)]]

    def v4(bk, n=4):
        return bk.ap.rearrange("p (a b) -> p a b", a=n) if n == 4 else bk[:, 0:n * 128].rearrange(
            "p (a b) -> p a b", a=n)

    issue_upto(T_k)
    for cc in range(2):
        bks = [nbank(), nbank(), nbank()]
        fm_mm(T_k, cc, 16, hT_rhs, hT_main_rngs, bks, [512, 512, 256])
        for j in range(2):
            S.op("dve", lambda e, j=j, cc=cc, bk=bks[j]: e.tensor_copy(out=kT[:, MAIN[j], cc, :], in_=v4(bk)),
                 reads=[bk.rng for bk in [bks[j]]], writes=[sub(kT, (2 + 4 * j) * 512, (6 + 4 * j) * 512)])
        S.op("dve", lambda e, cc=cc, bk=bks[2]: e.tensor_copy(out=kT[:, 0:2, cc, :], in_=v4(bk, 2)),
             reads=[bks[2].rng], writes=[sub(kT, 0, 1024)])
    issue_upto(T_v)
    wv = tiles[T_v][1]
    for p in range(NPOS):
        bk = nbank()
        for kc in range(16):
            S.op("pe", lambda e, kc=kc, p=p, bk=bk: e.matmul(bk[:, 0:256], lhsT=hT[:, p, kc, :], rhs=wv[:, kc, :],
                                                             start=(kc == 0), stop=(kc == 15)),
                 reads=[wv.rng, hT_rng(p)], writes=[sub(bk, 0, 1024)])
        if p % 2 == 0:
            S.op("act", lambda e, p=p, bk=bk: e.activation(out=Vt[:, p, :], in_=bk[:, 0:256], func=AF.Copy),
                 reads=[sub(bk, 0, 1024)], writes=[sub(Vt, p * 512, (p + 1) * 512)])
        else:
            S.op("dve", lambda e, p=p, bk=bk: e.tensor_copy(out=Vt[:, p, :], in_=bk[:, 0:256]),
                 reads=[sub(bk, 0, 1024)], writes=[sub(Vt, p * 512, (p + 1) * 512)])
    def q_wr(j, hd):
        return [sub(qT, b_ * 2048 + hd * 256, b_ * 2048 + (hd + 1) * 256) for b_ in range(4 * j, 4 * j + 4)]

    for qi in range(2):
        issue_upto(T_q[qi])
        for cc in range(2):
            hd = 2 * qi + cc
            bks = [nbank(), nbank()]
            fm_mm(T_q[qi], cc, 16, hT_rhs, hT_main_rngs, bks, [512, 512])
            for j in range(2):
                S.op("dve", lambda e, j=j, hd=hd, bk=bks[j]: e.tensor_copy(out=qT[:, 4 * j:4 * j + 4, hd, :],
                                                                           in_=v4(bk)),
                     reads=[bks[j].rng], writes=q_wr(j, hd))
    for gi in range(4):
        issue_upto(T_ga[gi])
        for cc in range(2):
            hd = 2 * gi + cc
            bks = [nbank(), nbank()]
            fm_mm(T_ga[gi], cc, 16, hT_rhs, hT_main_rngs, bks, [512, 512])
            for j in range(2):
                S.op("act", lambda e, j=j, hd=hd, bk=bks[j]: e.activation(out=sga[:, 4 * j:4 * j + 4, hd, :],
                                                                          in_=v4(bk), func=AF.Silu),
                     reads=[bks[j].rng], writes=[sub(sga, 4 * j * 2048, (4 * j + 4) * 2048)])
    dbg_dump("qT", qT, qT.ap.rearrange("p a b c -> p (a b c)"))
    dbg_dump("kT", kT, kT.ap.rearrange("p a b c -> p (a b c)"))
    dbg_dump("Vt", Vt, Vt.ap.rearrange("p a b -> p (a b)"))
    dbg_dump("sga", sga, sga.ap.rearrange("p a b c -> p (a b c)"))

    if stop == "pb1":
        return finish_early()
    for hh in range(8):
        S.op("dve", lambda e, hh=hh: e.tensor_copy(out=sk_hi[0:1, hh * 128:(hh + 1) * 128],
                                                   in_=expsink[0:1, hh:hh + 1].to_broadcast([1, 128])),
             reads=[expsink.rng], writes=[sub(sk_hi, hh * 256, (hh + 1) * 256)])
    S.op("dve", lambda e: e.tensor_copy(out=sk_f[0:1, :], in_=sk_hi[0:1, :]), reads=[sk_hi.rng], writes=[sk_f.rng])
    for hh in range(8):
        S.op("dve", lambda e, hh=hh: e.tensor_scalar(out=sk_lo[0:1, hh * 128:(hh + 1) * 128],
                                                     in0=sk_f[0:1, hh * 128:(hh + 1) * 128],
                                                     scalar1=-1.0, scalar2=expsink[0:1, hh:hh + 1],
                                                     op0=ALU.mult, op1=ALU.add),
             reads=[sk_f.rng, expsink.rng], writes=[sub(sk_lo, hh * 256, (hh + 1) * 256)])

    its = [(i, kvh) for kvh in range(2) for i in range(8)]

    def pc_geo(it):
        i, kvh = its[it]
        kp = [(i + 1) if i > 0 else 0, i + 2, (i + 3) if i < 7 else 1]
        bi = [3 if i == 0 else 0, 1, 4 if i == 7 else 2]
        sbk = [banks[c_] for c_ in range(3)]
        hs = slice(kvh * 4, kvh * 4 + 4)
        return i, kvh, kp, bi, sbk, hs, PT[it % 2]

    def pc_qk(it):
        i, kvh, kp, bi, sbk, hs, pts = pc_geo(it)
        for c_ in range(3):
            S.op("pe", lambda e, c_=c_, kvh=kvh, i=i, kp=kp, sbk=sbk, hs=hs:
                 e.matmul(sbk[c_].ap, lhsT=kT[:, kp[c_], kvh, :], rhs=qT[:, i, hs, :], start=True, stop=True),
                 reads=[sub(kT, kp[c_] * 512, (kp[c_] + 1) * 512),
                        sub(qT, i * 2048 + kvh * 1024, i * 2048 + (kvh + 1) * 1024)],
                 writes=[sbk[c_].rng])

    def pc_sm(it):
        i, kvh, kp, bi, sbk, hs, pts = pc_geo(it)
        for c_ in range(3):
            S.op("dve", lambda e, c_=c_, sbk=sbk, bi=bi, hs=hs:
                 e.scalar_tensor_tensor(out=scb[c_].ap, in0=v4(sbk[c_]), scalar=QS, in1=biasm[:, bi[c_], hs, :],
                                        op0=ALU.mult, op1=ALU.add),
                 reads=[sbk[c_].rng, biasm.rng], writes=[scb[c_].rng])
            S.op("act", lambda e, c_=c_, pts=pts: e.activation(out=pts[c_].ap, in_=scb[c_].ap, func=AF.Exp),
                 reads=[scb[c_].rng], writes=[pts[c_].rng])

    def pc_pv(it):
        i, kvh, kp, bi, sbk, hs, pts = pc_geo(it)
        obk, dbk = banks[3 + it % 2], banks[5 + it % 2]
        for c_ in range(3):
            S.op("pe", lambda e, c_=c_, kvh=kvh, kp=kp, pts=pts, obk=obk:
                 e.matmul(obk.ap, lhsT=Vt[:, kp[c_], kvh * 128:(kvh + 1) * 128],
                          rhs=pts[c_].ap.rearrange("p a b -> p (a b)"), start=(c_ == 0), stop=(c_ == 2)),
                 reads=[sub(Vt, kp[c_] * 512, (kp[c_] + 1) * 512), pts[c_].rng], writes=[obk.rng])
        for c_ in range(3):
            S.op("pe", lambda e, c_=c_, pts=pts, dbk=dbk:
                 e.matmul(dbk.ap, lhsT=onesb.ap, rhs=pts[c_].ap.rearrange("p a b -> p (a b)"),
                          start=(c_ == 0), stop=False),
                 reads=[onesb.rng, pts[c_].rng], writes=[dbk.rng])
        for j_, skr in enumerate((sk_hi, sk_lo)):
            S.op("pe", lambda e, skr=skr, kvh=kvh, dbk=dbk, j_=j_:
                 e.matmul(dbk.ap, lhsT=onesb[0:1, :], rhs=skr[0:1, kvh * 512:(kvh + 1) * 512],
                          start=False, stop=(j_ == 1)),
                 reads=[onesb.rng, skr.rng], writes=[dbk.rng])

    def pc_fin(it):
        i, kvh, kp, bi, sbk, hs, pts = pc_geo(it)
        obk, dbk = banks[3 + it % 2], banks[5 + it % 2]
        dn = den2[it % 2]
        S.op("act", lambda e, dbk=dbk, dn=dn: e.activation(out=dn.ap, in_=dbk.ap, func=AF.Ln), reads=[dbk.rng],
             writes=[dn.rng])
        S.op("act", lambda e, dn=dn: e.activation(out=dn.ap, in_=dn.ap, func=AF.Exp, scale=-1.0), reads=[dn.rng],
             writes=[dn.rng])
        yrng = sub(yaT, i * 2048 + kvh * 1024, i * 2048 + (kvh + 1) * 1024)
        S.op("dve", lambda e, obk=obk, dn=dn, i=i, hs=hs:
             e.tensor_tensor(out=yaT[:, i, hs, :], in0=v4(obk), in1=dn.ap.rearrange("p (a b) -> p a b", a=4),
                             op=ALU.mult),
             reads=[obk.rng, dn.rng], writes=[yrng])
        S.op("pool", lambda e, i=i, hs=hs: e.tensor_tensor(out=yaT[:, i, hs, :], in0=yaT[:, i, hs, :],
                                                           in1=sga[:, i, hs, :], op=ALU.mult),
             reads=[yrng, sub(sga, i * 2048, (i + 1) * 2048)], writes=[yrng])

    def pc_q(k):
        qi, cc, tt = 2 + k // 4, (k // 2) % 2, k % 2
        if k % 4 == 0:
            issue_upto(T_q[qi])
        hd = 2 * qi + cc
        bk = banks[7]
        fm_mm(T_q[qi], cc, 16, lambda j, kc, tt=tt: hT[:, MAIN[tt], kc, :], [hT_main_rngs[tt]], [bk], [512])
        S.op("dve", lambda e, tt=tt, hd=hd, bk=bk: e.tensor_copy(out=qT[:, 4 * tt:4 * tt + 4, hd, :], in_=v4(bk)),
             reads=[bk.rng], writes=q_wr(tt, hd))

    pc_qk(0)
    pc_sm(0)
    for it in range(16):
        if it < 8:
            pc_q(it)
        if it + 1 < 16:
            pc_qk(it + 1)
        pc_pv(it)
        if it + 1 < 16:
            pc_sm(it + 1)
        pc_fin(it)
    dbg_dump("yaT", yaT, yaT.ap.rearrange("p a b c -> p (a b c)"))

    if stop == "pc":
        return finish_early()
    def halo16(j, kc):
        if j < 2:
            return hT[:, MAIN[j], kc, :]
        st_ = kc * 128 + 120
        return hT_flat[:, st_: st_ + 2 * 1928].rearrange("p (a b) -> p a b", a=2)[:, :, 0:8]

    for g in range(4):
        w_ = 2 ** (g + 1)
        issue_upto(T_u[g])
        for cc in range(2):
            bks = [nbank(), nbank(), nbank()]
            fm_mm(T_u[g], cc, 16, halo16, hT_main_rngs, bks, [512, 512, 16])
            u = ub[cc]
            S.op("dve", lambda e, u=u, bk=bks[0]: e.tensor_copy(out=u[:, 8:520], in_=bk.ap),
                 reads=[bks[0].rng], writes=[u.rng])
            S.op("act", lambda e, u=u, bk=bks[1]: e.activation(out=u[:, 520:1032], in_=bk.ap, func=AF.Copy),
                 reads=[bks[1].rng], writes=[u.rng])
            S.op("dve", lambda e, u=u, bk=bks[2]: e.tensor_scalar(out=u[:, 0:8], in0=bk[:, 0:8],
                                                                   scalar1=pflag[:, 0:1], scalar2=None, op0=ALU.mult),
                 reads=[sub(bks[2], 0, 64), pflag.rng], writes=[u.rng])
            S.op("dve", lambda e, u=u, bk=bks[2]: e.tensor_scalar(out=u[:, 1032:1040], in0=bk[:, 8:16],
                                                                   scalar1=pflag[:, 1:2], scalar2=None, op0=ALU.mult),
                 reads=[sub(bks[2], 0, 64), pflag.rng], writes=[u.rng])
            src = u
            chain = [(1, 1040, 0, 1039, 1, 1040), (2, 1039, 1, 1038, 3, 1040), (4, 1037, 2, 1035, 6, 1039),
                     (8, 1033, 4, 1029, 12, 1037)]
            for lvl in range(g + 1):
                dst = pa if lvl % 2 == 0 else pb
                o0, o1, a0, a1, b0, b1 = chain[lvl]
                S.op("dve", lambda e, dst=dst, src=src, o0=o0, o1=o1, a0=a0, a1=a1, b0=b0, b1=b1:
                     e.tensor_tensor(out=dst[:, o0:o1], in0=src[:, a0:a1], in1=src[:, b0:b1], op=ALU.add),
                     reads=[src.rng], writes=[dst.rng])
                src = dst
            Wb = src
            S.op("dve", lambda e, Wb=Wb, u=u, cc=cc, w_=w_:
                 e.scalar_tensor_tensor(out=pooled[:, cc, :], in0=Wb[:, 8:1032], scalar=1.0 / w_, in1=u[:, 8:1032],
                                        op0=ALU.mult, op1=ALU.subtract),
                 reads=[Wb.rng, u.rng], writes=[sub(pooled, cc * 2048, (cc + 1) * 2048)])
            for (e0, c0_, p0_) in ((8, 0, 0), (1024, 8, 1016)):
                S.op("pool", lambda e, Wb=Wb, e0=e0, c0_=c0_, g=g:
                     e.tensor_tensor(out=etmp[:, 0:8], in0=Wb[:, e0:e0 + 8], in1=cinv[:, g, c0_:c0_ + 8], op=ALU.mult),
                     reads=[Wb.rng, cinv.rng], writes=[etmp.rng])
                S.op("pool", lambda e, u=u, e0=e0, p0_=p0_, cc=cc:
                     e.tensor_tensor(out=pooled[:, cc, p0_:p0_ + 8], in0=etmp[:, 0:8], in1=u[:, e0:e0 + 8],
                                     op=ALU.subtract),
                     reads=[etmp.rng, u.rng], writes=[sub(pooled, cc * 2048, (cc + 1) * 2048)])
        issue_upto(T_gp[g])
        for cc in range(2):
            bks = [nbank(), nbank()]
            fm_mm(T_gp[g], cc, 16, hT_rhs, hT_main_rngs, bks, [512, 512])
            for j in range(2):
                S.op("act", lambda e, j=j, cc=cc, bk=bks[j]: e.activation(out=sgp[:, cc, j * 512:(j + 1) * 512],
                                                                          in_=bk.ap, func=AF.Silu),
                     reads=[bks[j].rng], writes=[sub(sgp, cc * 2048, (cc + 1) * 2048)])
        for dc in range(2):
            for tt in range(2):
                bk = nbank()
                for cch in range(2):
                    S.op("pe", lambda e, g=g, dc=dc, tt=tt, cch=cch, bk=bk:
                         e.matmul(bk.ap, lhsT=wpool[:, g, cch, dc * 128:(dc + 1) * 128],
                                  rhs=pooled[:, cch, tt * 512:(tt + 1) * 512], start=(cch == 0), stop=(cch == 1)),
                         reads=[wpool.rng, sub(pooled, cch * 2048, (cch + 1) * 2048)], writes=[bk.rng])
                kcx = 2 * g + dc
                S.op("dve", lambda e, kcx=kcx, dc=dc, tt=tt, bk=bk:
                     e.scalar_tensor_tensor(out=ypT[:, 4 * tt:4 * tt + 4, kcx, :], in0=v4(bk),
                                            scalar=pscT[:, kcx:kcx + 1],
                                            in1=sgp[:, dc, tt * 512:(tt + 1) * 512].rearrange("p (a b) -> p a b", a=4),
                                            op0=ALU.mult, op1=ALU.mult),
                     reads=[bk.rng, pscT.rng, sub(sgp, dc * 2048, (dc + 1) * 2048)],
                     writes=[sub(ypT, 4 * tt * 2048, (4 * tt + 4) * 2048)])
    dbg_dump("ypT", ypT, ypT.ap.rearrange("p a b c -> p (a b c)"))

    if stop == "pd":
        return finish_early()
    def ya_rhs(j, kc):
        return yaT[:, 4 * j:4 * j + 4, kc, :]

    def yp_rhs(j, kc):
        return ypT[:, 4 * j:4 * j + 4, kc, :]

    ya_rngs = [[sub(yaT, 0, 8192)], [sub(yaT, 8192, 16384)]]
    yp_rngs = [[sub(ypT, 0, 8192)], [sub(ypT, 8192, 16384)]]
    for st in range(8):
        ta, tp, tma, tmp_ = T_e[st]
        issue_upto(tmp_, ta)
        for occ in range(2):
            oc = 2 * st + occ
            fm_mm(tma, occ, 16, hT_rhs, hT_main_rngs, [banks[0], banks[1]], [512, 512])
            fm_mm(tmp_, occ, 16, hT_rhs, hT_main_rngs, [banks[2], banks[3]], [512, 512])
            fm_mm(ta, occ, 8, ya_rhs, ya_rngs, [banks[4], banks[5]], [512, 512])
            fm_mm(tp, occ, 8, yp_rhs, yp_rngs, [banks[6], banks[7]], [512, 512])
            for tt in range(2):
                S.op("act", lambda e, tt=tt, oc=oc: e.activation(out=siga[:, tt * 512:(tt + 1) * 512],
                                                                 in_=banks[tt].ap, func=AF.Sigmoid,
                                                                 bias=bmT[:, oc:oc + 1], scale=1.0),
                     reads=[banks[tt].rng, bmT.rng], writes=[sub(siga, tt * 2048, (tt + 1) * 2048)])
            for tt in range(2):
                S.op("act", lambda e, tt=tt, oc=oc: e.activation(out=sigp[:, tt * 512:(tt + 1) * 512],
                                                                 in_=banks[2 + tt].ap, func=AF.Sigmoid,
                                                                 bias=bmT[:, 16 + oc:17 + oc], scale=1.0),
                     reads=[banks[2 + tt].rng, bmT.rng], writes=[sub(sigp, tt * 2048, (tt + 1) * 2048)])
            for tt in range(2):
                S.op("dve", lambda e, tt=tt: e.tensor_tensor(out=t1[:, tt * 512:(tt + 1) * 512],
                                                             in0=banks[4 + tt].ap,
                                                             in1=siga[:, tt * 512:(tt + 1) * 512], op=ALU.mult),
                     reads=[banks[4 + tt].rng, sub(siga, tt * 2048, (tt + 1) * 2048)],
                     writes=[sub(t1, tt * 2048, (tt + 1) * 2048)])
                S.op("dve", lambda e, tt=tt: e.tensor_tensor(out=t2[:, tt * 512:(tt + 1) * 512],
                                                             in0=banks[6 + tt].ap,
                                                             in1=sigp[:, tt * 512:(tt + 1) * 512], op=ALU.mult),
                     reads=[banks[6 + tt].rng, sub(sigp, tt * 2048, (tt + 1) * 2048)],
                     writes=[sub(t2, tt * 2048, (tt + 1) * 2048)])
                S.op("pool", lambda e, tt=tt, oc=oc:
                     e.tensor_tensor(out=mergedT[:, 4 * tt:4 * tt + 4, oc, :],
                                     in0=t1[:, tt * 512:(tt + 1) * 512].rearrange("p (a b) -> p a b", a=4),
                                     in1=t2[:, tt * 512:(tt + 1) * 512].rearrange("p (a b) -> p a b", a=4), op=ALU.add),
                     reads=[sub(t1, tt * 2048, (tt + 1) * 2048), sub(t2, tt * 2048, (tt + 1) * 2048)],
                     writes=[sub(mergedT, 4 * tt * 4096, (4 * tt + 4) * 4096)])
    dbg_dump("mergedT", mergedT, mergedT.ap.rearrange("p a b c -> p (a b c)"))

    if stop == "pe":
        return finish_early()
    outs = []

    def pf_final(i):
        ob = obuf[i]
        x = xf[i % 2]
        DMA("sp", x.rng, x.ap, xh[128 + 128 * i: 256 + 128 * i, :], name="xf")
        S.op("dve", lambda e, i=i: e.tensor_reduce(out=sst[:, i:i + 1], in_=ssf[:, 4 * i:4 * i + 4], axis=AX.X,
                                                   op=ALU.add),
             reads=[sub(ssf, 16 * i, 16 * i + 16)], writes=[sub(sst, 4 * i, 4 * i + 4)])
        S.op("act", lambda e, i=i: e.activation(out=rstdf[:, i:i + 1], in_=sst[:, i:i + 1], func=AF.Sqrt,
                                                bias=epst[:, 0:1], scale=1.0 / D),
             reads=[sub(sst, 4 * i, 4 * i + 4), epst.rng], writes=[sub(rstdf, 4 * i, 4 * i + 4)])
        S.op("dve", lambda e, i=i: e.reciprocal(out=rstdf[:, i:i + 1], in_=rstdf[:, i:i + 1]),
             reads=[sub(rstdf, 4 * i, 4 * i + 4)], writes=[sub(rstdf, 4 * i, 4 * i + 4)])
        S.op("dve", lambda e, i=i, ob=ob: e.scalar_tensor_tensor(out=ob.ap, in0=ob.ap, scalar=rstdf[:, i:i + 1],
                                                                 in1=modGP.ap, op0=ALU.mult, op1=ALU.mult),
             reads=[ob.rng, sub(rstdf, 4 * i, 4 * i + 4), modGP.rng], writes=[ob.rng])
        S.op("pool" if i % 2 == 0 else "dve",
             lambda e, ob=ob, x=x: e.tensor_tensor(out=ob.ap, in0=ob.ap, in1=x.ap, op=ALU.add),
             reads=[ob.rng, x.rng], writes=[ob.rng])
        outs.append(DMA("sp", None, out_d[128 * i:128 * (i + 1), :], ob.ap, reads=[ob.rng], name="out"))

    issue_upto(T_o[3], T_o[0])
    wos = [tiles[T_o[n]][1] for n in range(4)]

    def pf_evac(i, n, bk):
        ob = obuf[i]
        S.op("dve", lambda e, ob=ob, n=n, bk=bk: e.tensor_copy(out=ob[:, n * 512:(n + 1) * 512], in_=bk.ap),
             reads=[bk.rng], writes=[sub(ob, n * 2048, (n + 1) * 2048)])
        junkF = junkF2[n % 2]
        S.op("dve", lambda e, i=i, n=n, ob=ob, junkF=junkF:
             e.scalar_tensor_tensor(out=junkF.ap, in0=ob[:, n * 512:(n + 1) * 512], scalar=1.0,
                                    in1=ob[:, n * 512:(n + 1) * 512], op0=ALU.mult, op1=ALU.mult,
                                    accum_out=ssf[:, 4 * i + n:4 * i + n + 1]),
             reads=[sub(ob, n * 2048, (n + 1) * 2048)],
             writes=[junkF.rng, sub(ssf, 16 * i + 4 * n, 16 * i + 4 * n + 4)])

    for i in range(8):
        bk = nbank()
        for kc in range(16):
            S.op("pe", lambda e, kc=kc, i=i, bk=bk: e.matmul(bk.ap, lhsT=mergedT[:, i, kc, :], rhs=wos[0][:, kc, :],
                                                             start=(kc == 0), stop=(kc == 15)),
                 reads=[wos[0].rng, sub(mergedT, i * 4096, (i + 1) * 4096)], writes=[bk.rng])
        pf_evac(i, 0, bk)
    for i in range(8):
        bks = [nbank() for _ in range(3)]
        for kc in range(16):
            for n in range(1, 4):
                S.op("pe", lambda e, kc=kc, i=i, n=n, bks=bks: e.matmul(bks[n - 1].ap, lhsT=mergedT[:, i, kc, :],
                                                                         rhs=wos[n][:, kc, :], start=(kc == 0),
                                                                         stop=(kc == 15)),
                     reads=[wos[n].rng, sub(mergedT, i * 4096, (i + 1) * 4096)], writes=[bks[n - 1].rng])
        for n in range(1, 4):
            pf_evac(i, n, bks[n - 1])
        if i >= 1:
            pf_final(i - 1)
    pf_final(7)

    S.emit(nc, es)
    es.close()
    return nc


def _t5_bucket(rel):
    half, max_exact = 16, 8
    ret = np.where(rel > 0, half, 0)
    n = np.abs(rel)
    nf = np.maximum(n, 1).astype(np.float32)
    large = max_exact + (np.log(nf / max_exact) / math.log(128 / max_exact) * (half - max_exact)).astype(np.int32)
    large = np.minimum(large, half - 1)
    return ret + np.where(n < max_exact, n, large)


def _host_inputs(x, c, rel_bias_table, w_ada, b_ada, pre_norm_g, post_norm_g, w_in, attn_sink, w_pool_group,
                 pool_scale, w_branch_attn, w_branch_pool, w_merge, b_merge, w_out):
    f = lambda a: np.ascontiguousarray(np.asarray(a, dtype=np.float32))
    x2 = f(x)[0]
    xpad = np.zeros((S_TOT + 256, D), np.float32)
    xpad[128:128 + S_TOT] = x2
    key = np.arange(128)[:, None]
    qry = np.arange(128)[None, :]
    tab = f(rel_bias_table)
    tiles = []
    for cidx in (-1, 0, 1):
        rel = key + 128 * cidx - qry
        g = tab[_t5_bucket(rel)]
        g = np.transpose(g, (0, 2, 1))
        valid = (np.abs(rel) <= 128)[:, None, :]
        tiles.append(np.where(valid, g, np.float32(NEG)).astype(np.float32))
    allneg = np.full_like(tiles[0], NEG)
    shared = dict(
        cT=f(np.asarray(c)[0].reshape(16, 128).T), w_ada=f(w_ada)[0], b_ada=f(b_ada)[0][None, :],
        pre_g=f(pre_norm_g)[0][None, :], post_g=f(post_norm_g)[0][None, :], w_in=f(w_in)[0],
        w_pool=f(w_pool_group)[0], w_a=f(w_branch_attn)[0], w_p=f(w_branch_pool)[0], w_m=f(w_merge)[0],
        w_out=f(w_out)[0], bmT=f(f(b_merge)[0].reshape(32, 128).T), pscT=f(f(pool_scale)[0].reshape(8, 128).T),
        sinkB=f(np.broadcast_to(f(attn_sink)[0][None, :], (128, 8))), identf=np.eye(128, dtype=np.float32))
    maps = []
    for k in range(NCORES):
        m = dict(shared)
        m["xh"] = np.ascontiguousarray(xpad[k * T: k * T + T + 256])
        first = allneg if k == 0 else tiles[0]
        last = allneg if k == NCORES - 1 else tiles[2]
        m["biasT"] = np.ascontiguousarray(np.stack([tiles[0], tiles[1], tiles[2], first, last], axis=1)
                                          .reshape(128, 5 * 8 * 128))
        m["pflag"] = np.ascontiguousarray(np.broadcast_to(
            np.array([0.0 if k == 0 else 1.0, 0.0 if k == NCORES - 1 else 1.0], np.float32)[None, :], (128, 2)))
        ci = np.zeros((4, 16), np.float32)
        for g in range(4):
            w = 2 ** (g + 1)
            for j in range(16):
                tl = j if j < 8 else T - 16 + j
                gi = k * T + tl
                lo, hi = max(gi - w // 2, 0), min(gi + w // 2, S_TOT)
                ci[g, j] = 1.0 / float(hi - lo)
        m["cinv"] = np.ascontiguousarray(np.broadcast_to(ci.reshape(1, 64), (128, 64)))
        maps.append(m)
    return maps


_NC_CACHE = {}


def kernel(**inputs):
    maps = _host_inputs(**inputs)
    if "nc" not in _NC_CACHE:
        _NC_CACHE["nc"] = build_program()
    nc = _NC_CACHE["nc"]
    res = run_bass_kernel_spmd(nc, maps, core_ids=list(range(NCORES)))
    out = np.concatenate([np.asarray(r["out"], dtype=np.float32) for r in res.results], axis=0)
    return out.reshape(1, S_TOT, D)
```

```python
import math
import os as _os
from contextlib import ExitStack

import numpy as np
import concourse.bass as bass
import concourse.mybir as mybir
from concourse.bass_utils import run_bass_kernel_spmd

F32, BF16, F32R = mybir.dt.float32, mybir.dt.bfloat16, mybir.dt.float32r
AF = mybir.ActivationFunctionType
ALU = mybir.AluOpType
AX = mybir.AxisListType

NCORES = 8
S_TOT, D = 8192, 2048
T = S_TOT // NCORES
NPOS = 10
EPS = 1e-6
NEG = -30000.0
IN_W = 4608
QS = 128 ** -0.5
ENGS = ("sp", "act", "pool", "dve", "pe")
NRING = 16


class Op:
    __slots__ = ("eng", "fn", "deps", "dma", "sem", "val", "users", "name")

    def __init__(self, eng, fn, dma, name):
        self.eng, self.fn, self.dma, self.name = eng, fn, dma, name
        self.deps = set()
        self.sem = None
        self.val = 0
        self.users = 0


class Rec:
    __slots__ = ("lo", "hi", "w", "r")

    def __init__(self, lo, hi, w):
        self.lo, self.hi, self.w, self.r = lo, hi, w, {}


class Sched:
    def __init__(self):
        self.q = {e: [] for e in ENGS}
        self.recs = {"sb": [], "ps": []}
        self.ndma = {e: [] for e in ENGS}

    @staticmethod
    def _ordered(d, o):
        return (not d.dma) and (not o.dma) and d.eng == o.eng and d.eng == "pe"

    def op(self, eng, fn, reads=(), writes=(), dma=False, name=""):
        o = Op(eng, fn, dma, name)
        deps = set()
        reads = [(sp, lo // 2048 * 2048, -(-hi // 2048) * 2048) if sp == "ps" else (sp, lo, hi)
                 for (sp, lo, hi) in reads]
        writes = [(sp, lo // 2048 * 2048, -(-hi // 2048) * 2048) if sp == "ps" else (sp, lo, hi)
                  for (sp, lo, hi) in writes]
        for (sp, lo, hi) in reads:
            for r in self.recs[sp]:
                if r.lo < hi and lo < r.hi:
                    if r.w is not None and r.w is not o:
                        d = r.w
                        if not (d.eng == "pe" and eng == "pe" and not dma):
                            deps.add(d)
                    if sp == "ps":
                        for x in r.r.values():
                            if x is not o and x.eng != eng:
                                deps.add(x)
                    r.r[id(o) if dma else eng] = o
        for (sp, lo, hi) in writes:
            new = []
            for r in self.recs[sp]:
                if r.lo < hi and lo < r.hi:
                    if r.w is not None and r.w is not o and not self._ordered(r.w, o):
                        deps.add(r.w)
                    for x in r.r.values():
                        if x is not o and not self._ordered(x, o):
                            deps.add(x)
                    if lo <= r.lo and r.hi <= hi:
                        continue
                new.append(r)
            new.append(Rec(lo, hi, o))
            self.recs[sp] = new
        if dma:
            ring = self.ndma[eng]
            if len(ring) >= NRING:
                deps.add(ring[len(ring) - NRING])
            ring.append(o)
        deps.discard(o)
        o.deps = deps
        for d in deps:
            d.users += 1
        self.q[eng].append(o)
        return o

    def emit(self, nc, es):
        engsem = {e: es.enter_context(nc.semaphore("s_" + e)) for e in ENGS}
        rings = {e: [es.enter_context(nc.semaphore("d_%s%d" % (e, i))) for i in range(NRING)]
                 for e in ENGS if self.ndma[e]}
        for e in ENGS:
            cnt = 0
            for o in self.q[e]:
                if o.dma:
                    continue
                if o.users > 0:
                    cnt += 1
                    o.sem, o.val = engsem[e], cnt
        for e in ENGS:
            for i, o in enumerate(self.ndma[e]):
                o.sem, o.val = rings[e][i % NRING], 16 * (i // NRING + 1)

        def run(e, eng):
            waited = {}
            for o in self.q[e]:
                need = {}
                for d in o.deps:
                    k = id(d.sem)
                    if k not in need or need[k][1] < d.val:
                        need[k] = (d.sem, d.val)
                for k, (sem, val) in need.items():
                    if waited.get(k, 0) >= val:
                        continue
                    waited[k] = val
                    eng.wait_ge(sem, val)
                ins = o.fn(eng)
                if o.dma:
                    ins.then_inc(o.sem, 16)
                elif o.users > 0:
                    ins.then_inc(o.sem, 1)
            for i, sem in enumerate(rings.get(e, [])):
                n = len(self.ndma[e])
                uses = (n - i + NRING - 1) // NRING if n > i else 0
                if uses > 0:
                    eng.wait_ge(sem, 16 * uses)

        block = es.enter_context(nc.Block())
        block_engines = {"sp": block.sync, "act": block.scalar, "pool": block.gpsimd, "dve": block.vector,
                         "pe": block.tensor}
        for e in ENGS:
            block_engines[e](lambda eng, e=e: run(e, eng))


class Buf:
    def __init__(self, space, ap, lo, hi):
        self.space, self.ap, self.lo, self.hi = space, ap, lo, hi

    @property
    def rng(self):
        return (self.space, self.lo, self.hi)

    def __getitem__(self, k):
        return self.ap[k]


def build_program(debug=(), stop=None):
    nc = bass.Bass("TRN2", target_bir_lowering=False)

    def din(name, shape):
        return nc.dram_tensor(name, list(shape), F32, kind="ExternalInput").ap()

    xh = din("xh", [T + 256, D])
    cT_d = din("cT", [128, 16])
    w_ada = din("w_ada", [D, 3 * D])
    b_ada = din("b_ada", [1, 3 * D])
    pre_g = din("pre_g", [1, D])
    post_g = din("post_g", [1, D])
    w_in = din("w_in", [D, IN_W])
    w_pool = din("w_pool", [4, 256, 256])
    w_a = din("w_a", [1024, D])
    w_p = din("w_p", [1024, D])
    w_m = din("w_m", [D, 2 * D])
    w_out = din("w_out", [D, D])
    biasT = din("biasT", [128, 5 * 8 * 128])
    bmT_d = din("bmT", [128, 32])
    pscT_d = din("pscT", [128, 8])
    sinkB_d = din("sinkB", [128, 8])
    pflag_d = din("pflag", [128, 2])
    cinv_d = din("cinv", [128, 64])
    ident_d = din("identf", [128, 128])
    out_d = nc.dram_tensor("out", [T, D], F32, kind="ExternalOutput").ap()
    dbg_d = {}
    for (nm, shp, dt_) in debug:
        dbg_d[nm] = nc.dram_tensor("dbg_" + nm, list(shp), dt_, kind="ExternalOutput").ap()

    S = Sched()
    es = ExitStack()
    ARENA_B = 207872
    arena = es.enter_context(nc.sbuf_tensor("arena", [128, ARENA_B // 4], F32))
    psum = es.enter_context(nc.psum_tensor("psum", [128, 4096], F32))

    def sb(off, dtype, shape):
        n = int(np.prod(shape))
        esz = 2 if dtype == BF16 else 4
        assert off % 4 == 0 and (n * esz) % 4 == 0
        ap = arena[:, off // 4: off // 4 + (n * esz) // 4]
        if dtype != F32:
            ap = ap.bitcast(dtype)
        if len(shape) == 2:
            ap = ap.rearrange("p (a b) -> p a b", a=shape[0])
        elif len(shape) == 3:
            ap = ap.rearrange("p (a b c) -> p a b c", a=shape[0], b=shape[1])
        assert off + n * esz <= ARENA_B
        return Buf("sb", ap, off, off + n * esz)

    def sub(buf, lo, hi):
        return (buf.space, buf.lo + lo, buf.lo + hi)

    def bank(k, dtype=F32):
        ap = psum[:, 512 * k: 512 * (k + 1)]
        if dtype != F32:
            ap = ap.bitcast(dtype)
        return Buf("ps", ap, 2048 * k, 2048 * (k + 1))

    banks = [bank(k) for k in range(8)]

    A0, B0, W0, M0, X0, C0 = 0, 40960, 61440, 110592, 135168, 194560
    hT = sb(A0, BF16, (NPOS, 16, 128))
    hT_flat = sb(A0, BF16, (NPOS * 16 * 128,))

    def hT_rng(p, half=None):
        if half is None:
            return sub(hT, p * 4096, (p + 1) * 4096)
        return sub(hT, p * 4096 + half * 2048, p * 4096 + (half + 1) * 2048)

    biasm = sb(B0, F32, (5, 8, 128))
    postgB = sb(B0, F32, (D,))
    modS = sb(M0, F32, (D,))
    modG = sb(M0 + 8192, F32, (D,))
    modGP = sb(M0 + 16384, F32, (D,))
    modB = sb(M0, F32, (3 * D,))
    yaT = sb(M0, BF16, (8, 8, 128))
    sk_f = sb(M0, F32, (1024,))
    scB = sb(X0, BF16, (16, 128))
    xt = [sb(X0 + 8192, F32, (D,)), sb(X0 + 16384, F32, (D,)), sb(B0 + 8192, F32, (D,)), sb(X0 + 45056, F32, (D,))]
    tbuf = sb(X0 + 24576, F32, (D,))
    hb = [sb(X0 + 32768, BF16, (D,)), sb(X0 + 36864, BF16, (D,))]
    junkA2 = [sb(X0 + 40960, BF16, (D,)), sb(X0 + 53248, BF16, (D,))]
    pregB = sb(X0 + 45056, F32, (D,))
    ada_slot = [sb(W0 + 8192 * i, BF16, (16, 256)) for i in range(6)]
    qT = sb(X0, BF16, (8, 8, 128))
    sga = sb(X0 + 16384, BF16, (8, 8, 128))
    kT = sb(X0 + 32768, BF16, (NPOS, 2, 128))
    Vt = sb(X0 + 37888, BF16, (NPOS, 256))
    scb = [sb(X0 + 43008 + 2048 * i, F32, (4, 128)) for i in range(3)]
    PT = [[sb(X0 + 49152 + 3072 * j + 1024 * c, BF16, (4, 128)) for c in range(3)] for j in range(2)]
    den = sb(X0 + 55296, F32, (512,))
    den2 = [den, sb(X0 + 57344, F32, (512,))]
    osb = sb(X0 + 57344, F32, (4, 128))
    ypT = sb(X0, BF16, (8, 8, 128))
    ub = [sb(X0 + 16384, F32, (1040,)), sb(X0 + 20544, F32, (1040,))]
    pa = sb(X0 + 24704, F32, (1040,))
    pb = sb(X0 + 28864, F32, (1040,))
    sgp = sb(X0 + 33024, BF16, (2, 1024))
    pooled = sb(X0 + 37120, BF16, (2, 1024))
    etmp = sb(X0 + 41216, F32, (16,))
    mergedT = sb(X0 + 16384, BF16, (8, 16, 128))
    siga = sb(B0, F32, (1024,))
    sigp = sb(B0 + 4096, F32, (1024,))
    t1 = sb(B0 + 8192, F32, (1024,))
    t2 = sb(B0 + 12288, F32, (1024,))
    obuf = [sb(A0 + 8192 * i, F32, (D,)) for i in range(5)] + \
           [sb(B0, F32, (D,)), sb(B0 + 8192, F32, (D,)), sb(X0 + 49152, F32, (D,))]
    xf = [sb(M0, F32, (D,)), sb(M0 + 8192, F32, (D,))]
    junkF2 = [sb(X0 + 57344, BF16, (512,)), sb(X0 + 58368, BF16, (512,))]
    c = C0
    def calloc(dtype, shape):
        nonlocal c
        b = sb(c, dtype, shape)
        c = b.hi
        c = (c + 31) // 32 * 32
        return b
    ident = calloc(BF16, (128,))
    onesb = calloc(BF16, (128,))
    onesf = calloc(F32, (128,))
    identf = calloc(F32, (128,))
    cT = calloc(F32, (16,))
    scs = calloc(F32, (16,))
    bmT = calloc(F32, (32,))
    pscT = calloc(F32, (8,))
    sinkB = calloc(F32, (8,))
    expsink = calloc(F32, (8,))
    pflag = calloc(F32, (2,))
    cinv = calloc(F32, (4, 16))
    ss = calloc(F32, (16,))
    rstd = calloc(F32, (16,))
    ssf = calloc(F32, (32,))
    sst = calloc(F32, (8,))
    rstdf = calloc(F32, (8,))
    epst = calloc(F32, (2,))
    wpool = calloc(BF16, (4, 2, 256))
    sk_hi = calloc(BF16, (1024,))
    sk_lo = calloc(BF16, (1024,))
    wslot = [(W0 + 8192 * i) for i in range(6)]

    def DMA(eng, out_buf_rng, out_ap, in_ap, reads=(), name="dma"):
        return S.op(eng, lambda e: e.dma_start(out=out_ap, in_=in_ap), reads=list(reads),
                    writes=[out_buf_rng] if out_buf_rng is not None else [], dma=True, name=name)

    def dbg_dump(nm, buf, ap2d):
        if nm in dbg_d:
            DMA("sp", None, dbg_d[nm], ap2d, reads=[buf.rng], name="dbg")

    def finish_early():
        DMA("sp", None, out_d[0:128, :], modB[:, 0:D], reads=[modB.rng], name="out")
        S.emit(nc, es)
        es.close()
        return nc

    DMA("sp", identf.rng, identf.ap, ident_d)
    DMA("sp", cT.rng, cT.ap, cT_d)
    S.op("dve", lambda e: e.memset(onesf.ap, 1.0), writes=[onesf.rng])
    S.op("dve", lambda e: e.memset(onesb.ap, 1.0), writes=[onesb.rng])
    S.op("dve", lambda e: e.memset(ss.ap, 0.0), writes=[ss.rng])
    S.op("dve", lambda e: e.memset(ssf.ap, 0.0), writes=[ssf.rng])
    S.op("dve", lambda e: e.memset(epst.ap, EPS), writes=[epst.rng])
    S.op("dve", lambda e: e.tensor_copy(out=ident.ap, in_=identf.ap), reads=[identf.rng], writes=[ident.rng])
    S.op("act", lambda e: e.activation(out=scs.ap, in_=cT.ap, func=AF.Silu), reads=[cT.rng], writes=[scs.rng])
    scB_r = scB.ap
    for kc in range(16):
        S.op("dve", lambda e, kc=kc: e.tensor_scalar(out=scB_r[:, kc, :], in0=onesf.ap, scalar1=scs[:, kc:kc + 1],
                                                      scalar2=None, op0=ALU.mult),
             reads=[onesf.rng, scs.rng], writes=[sub(scB, kc * 256, (kc + 1) * 256)])
    def xrow(p):
        r = 0 if p == 0 else (9 if p == 1 else p - 1)
        return xh[r * 128:(r + 1) * 128, :]
    for p in range(3):
        DMA("sp", xt[p].rng, xt[p].ap, xrow(p), name="x")
    for j in range(3):
        DMA("sp", sub(modB, j * 8192, (j + 1) * 8192), modB[:, j * D:(j + 1) * D],
            b_ada[0:1, j * D:(j + 1) * D].partition_broadcast(128))
    DMA("sp", pregB.rng, pregB.ap, pre_g[0:1, :].partition_broadcast(128))
    DMA("sp", postgB.rng, postgB.ap, post_g[0:1, :].partition_broadcast(128))
    for (bf, dd) in ((bmT, bmT_d), (pscT, pscT_d), (sinkB, sinkB_d), (pflag, pflag_d)):
        DMA("sp", bf.rng, bf.ap, dd)
    DMA("sp", cinv.rng, cinv.ap, cinv_d.rearrange("p (a b) -> p a b", a=4))
    S.op("act", lambda e: e.activation(out=expsink.ap, in_=sinkB.ap, func=AF.Exp), reads=[sinkB.rng],
         writes=[expsink.rng])

    tiles = []
    state = {"slot": 0, "issued": 0}
    tenant = {}

    def add_tile(src, kc, ncols, off_in_slot=0, newslot=True, nslots=1, fixed=None):
        if fixed is not None:
            base_, prev_ = fixed
            b = sb(base_, BF16, (kc, ncols))
            tiles.append((src, b, prev_))
            return len(tiles) - 1
        if newslot:
            if nslots == 2 and state["slot"] % 2 == 1:
                state["slot"] += 1
            first = state["slot"] % 6
            state["slot"] += nslots
        else:
            first = (state["slot"] - 1) % 6
        base = wslot[first]
        b = sb(base + off_in_slot, BF16, (kc, ncols))
        idx = len(tiles)
        prev = -1
        for sl_ in range(first, first + nslots):
            cur, old = tenant.get(sl_, ([], -1))
            if newslot:
                p_ = max(cur) if cur else -1
                tenant[sl_] = ([idx], p_)
                prev = max(prev, p_)
            else:
                cur.append(idx)
                prev = max(prev, old)
        tiles.append((src, b, prev))
        return idx

    def issue_upto(n_last, n_first=None):
        if n_first is None:
            n_first = n_last
        lim = min(n_last + LA, len(tiles) - 1)
        while state["issued"] <= lim:
            src, b, prev = tiles[state["issued"]]
            if state["issued"] > n_last and prev >= n_first:
                break
            assert prev < n_first
            DMA("pool", b.rng, b.ap, src, name="w%d" % state["issued"])
            state["issued"] += 1
        assert state["issued"] > n_last

    w_in_v = w_in.rearrange("(kc p) n -> p kc n", p=128)
    w_m_v = w_m.rearrange("(kc p) n -> p kc n", p=128)
    w_a_v = w_a.rearrange("(kc p) n -> p kc n", p=128)
    w_p_v = w_p.rearrange("(kc p) n -> p kc n", p=128)
    w_out_v = w_out.rearrange("(kc p) n -> p kc n", p=128)

    def win_tile(c0):
        return add_tile(w_in_v[:, :, c0:c0 + 256], 16, 256)

    T_k = win_tile(1024)
    T_v = win_tile(1280)
    T_q = [win_tile(0), win_tile(256)]
    T_ga = [win_tile(1536 + 256 * i) for i in range(4)]
    T_q += [win_tile(512), win_tile(768)]
    T_u, T_gp = [], []
    for g in range(4):
        T_u.append(win_tile(2560 + 256 * g))
        T_gp.append(win_tile(3584 + 256 * g))
    T_e = []
    for st in range(8):
        c0 = st * 256
        ta = add_tile(w_a_v[:, :, c0:c0 + 256], 8, 256)
        tp = add_tile(w_p_v[:, :, c0:c0 + 256], 8, 256, off_in_slot=4096, newslot=False)
        tma = add_tile(w_m_v[:, :, c0:c0 + 256], 16, 256)
        tmp_ = add_tile(w_m_v[:, :, D + c0:D + c0 + 256], 16, 256)
        T_e.append((ta, tp, tma, tmp_))
    T_o = [add_tile(w_out_v[:, :, n * 512:(n + 1) * 512], 16, 512, nslots=2) for n in range(3)]
    T_o.append(add_tile(w_out_v[:, :, 3 * 512:4 * 512], 16, 512, fixed=(X0, T_e[-1][3])))
    LA = 4


    def issue_cap(k):
        while state["issued"] <= k:
            src, b, prev = tiles[state["issued"]]
            assert prev < 0
            DMA("pool", b.rng, b.ap, src, name="w%d" % state["issued"])
            state["issued"] += 1

    if stop == "setup":
        return finish_early()
    w_ada_v = w_ada.rearrange("(kc p) n -> p kc n", p=128)

    def p0_dma(j):
        sl = ada_slot[j % 6]
        DMA("pool", sl.rng, sl.ap, w_ada_v[:, :, j * 256:(j + 1) * 256], name="wada")

    def p0_consume(j):
        sl = ada_slot[j % 6]
        bk = banks[j % 2]
        for kc in range(16):
            S.op("pe", lambda e, kc=kc: e.matmul(bk[:, 0:256], lhsT=scB_r[:, kc, :], rhs=sl[:, kc, :],
                                                  start=(kc == 0), stop=(kc == 15)),
                 reads=[sl.rng, scB.rng], writes=[bk.rng])
        S.op("dve", lambda e: e.tensor_tensor(out=modB[:, j * 256:(j + 1) * 256], in0=bk[:, 0:256],
                                              in1=modB[:, j * 256:(j + 1) * 256], op=ALU.add),
             reads=[bk.rng, sub(modB, j * 1024, (j + 1) * 1024)],
             writes=[sub(modB, j * 1024, (j + 1) * 1024)])

    for j in range(6):
        p0_dma(j)
    for j in range(16):
        p0_consume(j)
        if j + 6 <= 21:
            p0_dma(j + 6)
    S.op("dve", lambda e: e.scalar_tensor_tensor(out=modG.ap, in0=modG.ap, scalar=1.0, in1=pregB.ap,
                                                 op0=ALU.add, op1=ALU.mult),
         reads=[modG.rng, pregB.rng], writes=[modG.rng])

    if stop == "p0":
        return finish_early()
    def pa_s1(p):
        x = xt[p % 4]
        if p >= 3:
            DMA("sp", x.rng, x.ap, xrow(p), name="x")
        junkA = junkA2[p % 2]
        S.op("act", lambda e: e.activation(out=junkA.ap, in_=x.ap, func=AF.Square, accum_out=ss[:, p:p + 1]),
             reads=[x.rng], writes=[junkA.rng, sub(ss, 4 * p, 4 * p + 4)])
        S.op("act", lambda e: e.activation(out=rstd[:, p:p + 1], in_=ss[:, p:p + 1], func=AF.Sqrt,
                                           bias=epst[:, 0:1], scale=1.0 / D),
             reads=[sub(ss, 4 * p, 4 * p + 4), epst.rng], writes=[sub(rstd, 4 * p, 4 * p + 4)])
        S.op("dve", lambda e: e.reciprocal(out=rstd[:, p:p + 1], in_=rstd[:, p:p + 1]),
             reads=[sub(rstd, 4 * p, 4 * p + 4)], writes=[sub(rstd, 4 * p, 4 * p + 4)])
        S.op("dve", lambda e: e.scalar_tensor_tensor(out=tbuf.ap, in0=x.ap, scalar=rstd[:, p:p + 1], in1=modG.ap,
                                                     op0=ALU.mult, op1=ALU.mult),
             reads=[x.rng, sub(rstd, 4 * p, 4 * p + 4), modG.rng], writes=[tbuf.rng])
        h = hb[p % 2]
        S.op("dve", lambda e: e.tensor_tensor(out=h[:, 0:1280], in0=tbuf[:, 0:1280], in1=modS[:, 0:1280], op=ALU.add),
             reads=[sub(tbuf, 0, 5120), sub(modS, 0, 5120)], writes=[sub(h, 0, 2560)])
        S.op("pool", lambda e: e.tensor_tensor(out=h[:, 1280:2048], in0=tbuf[:, 1280:2048], in1=modS[:, 1280:2048],
                                               op=ALU.add),
             reads=[sub(tbuf, 5120, 8192), sub(modS, 5120, 8192)], writes=[sub(h, 2560, 4096)])

    def pa_s2(p):
        h = hb[p % 2]
        k0 = 2 + 2 * (p % 2)
        pT = psum[:, 512 * k0: 512 * (k0 + 2)].bitcast(BF16)
        prng = ("ps", 2048 * k0, 2048 * (k0 + 2))
        for kc in range(16):
            S.op("pe", lambda e, kc=kc: e.transpose(pT[:, kc * 128:(kc + 1) * 128], h[:, kc * 128:(kc + 1) * 128],
                                                    ident.ap),
                 reads=[sub(h, kc * 256, (kc + 1) * 256), ident.rng], writes=[prng])
        S.op("act", lambda e: e.activation(out=hT[:, p, 0:8, :],
                                           in_=pT[:, 0:1024].rearrange("p (a b) -> p a b", a=8), func=AF.Copy),
             reads=[("ps", 2048 * k0, 2048 * (k0 + 1))], writes=[hT_rng(p, 0)])
        S.op("dve", lambda e: e.tensor_copy(out=hT[:, p, 8:16, :],
                                            in_=pT[:, 1024:2048].rearrange("p (a b) -> p a b", a=8)),
             reads=[("ps", 2048 * (k0 + 1), 2048 * (k0 + 2))], writes=[hT_rng(p, 1)])

    pa_s1(0)
    for p in range(NPOS):
        if p + 1 < NPOS:
            pa_s1(p + 1)
        pa_s2(p)
        if p == 2:
            p0_consume(16)
            p0_consume(17)
        if p == 4:
            p0_consume(18)
            p0_consume(19)
            p0_dma(22)
            p0_dma(23)
        if p == 6:
            p0_consume(20)
            p0_consume(21)
        if p == 8:
            issue_cap(3)
    p0_consume(22)
    p0_consume(23)
    S.op("dve", lambda e: e.tensor_tensor(out=modGP.ap, in0=modGP.ap, in1=postgB.ap, op=ALU.mult),
         reads=[modGP.rng, postgB.rng], writes=[modGP.rng])
    dbg_dump("hT", hT, hT_flat.ap)

    if stop == "pa":
        return finish_early()
    DMA("sp", biasm.rng, biasm.ap, biasT.rearrange("p (a b c) -> p a b c", a=5, b=8))
    DMA("pool", wpool.rng, wpool.ap, w_pool.rearrange("g (cc p) d -> p g cc d", p=128), name="wpool")

    bank_ctr = {"n": 0}

    def nbank():
        b = banks[bank_ctr["n"] % 8]
        bank_ctr["n"] += 1
        return b

    MAIN = (slice(2, 6), slice(6, 10))

    def fm_mm(ti, cc, kcs, rhs_fn, rhs_rngs, bks, widths):
        wb = tiles[ti][1]
        for kc in range(kcs):
            for j, bk in enumerate(bks):
                S.op("pe", lambda e, kc=kc, j=j, bk=bk: e.matmul(bk[:, 0:widths[j]],
                                                                  lhsT=wb[:, kc, cc * 128:(cc + 1) * 128],
                                                                  rhs=rhs_fn(j, kc),
                                                                  start=(kc == 0), stop=(kc == kcs - 1)),
                     reads=[wb.rng] + rhs_rngs[j], writes=[sub(bk, 0, widths[j] * 4)])

    def hT_rhs(j, kc):
        if j < 2:
            return hT[:, MAIN[j], kc, :]
        return hT[:, 0:2, kc, :]

    hT_main_rngs = [[sub(hT, 2 * 4096, 6 * 4096)], [sub(hT, 6 * 4096, 10 * 4096)], [sub(hT, 0, 2 * 4096)]]

    def v4(bk, n=4):
        return bk.ap.rearrange("p (a b) -> p a b", a=n) if n == 4 else bk[:, 0:n * 128].rearrange(
            "p (a b) -> p a b", a=n)

    issue_upto(T_k)
    for cc in range(2):
        bks = [nbank(), nbank(), nbank()]
        fm_mm(T_k, cc, 16, hT_rhs, hT_main_rngs, bks, [512, 512, 256])
        for j in range(2):
            S.op("dve", lambda e, j=j, cc=cc, bk=bks[j]: e.tensor_copy(out=kT[:, MAIN[j], cc, :], in_=v4(bk)),
                 reads=[bk.rng for bk in [bks[j]]], writes=[sub(kT, (2 + 4 * j) * 512, (6 + 4 * j) * 512)])
        S.op("dve", lambda e, cc=cc, bk=bks[2]: e.tensor_copy(out=kT[:, 0:2, cc, :], in_=v4(bk, 2)),
             reads=[bks[2].rng], writes=[sub(kT, 0, 1024)])
    issue_upto(T_v)
    wv = tiles[T_v][1]
    for p in range(NPOS):
        bk = nbank()
        for kc in range(16):
            S.op("pe", lambda e, kc=kc, p=p, bk=bk: e.matmul(bk[:, 0:256], lhsT=hT[:, p, kc, :], rhs=wv[:, kc, :],
                                                             start=(kc == 0), stop=(kc == 15)),
                 reads=[wv.rng, hT_rng(p)], writes=[sub(bk, 0, 1024)])
        if p % 2 == 0:
            S.op("act", lambda e, p=p, bk=bk: e.activation(out=Vt[:, p, :], in_=bk[:, 0:256], func=AF.Copy),
                 reads=[sub(bk, 0, 1024)], writes=[sub(Vt, p * 512, (p + 1) * 512)])
        else:
            S.op("dve", lambda e, p=p, bk=bk: e.tensor_copy(out=Vt[:, p, :], in_=bk[:, 0:256]),
                 reads=[sub(bk, 0, 1024)], writes=[sub(Vt, p * 512, (p + 1) * 512)])
    def q_wr(j, hd):
        return [sub(qT, b_ * 2048 + hd * 256, b_ * 2048 + (hd + 1) * 256) for b_ in range(4 * j, 4 * j + 4)]

    for qi in range(2):
        issue_upto(T_q[qi])
        for cc in range(2):
            hd = 2 * qi + cc
            bks = [nbank(), nbank()]
            fm_mm(T_q[qi], cc, 16, hT_rhs, hT_main_rngs, bks, [512, 512])
            for j in range(2):
                S.op("dve", lambda e, j=j, hd=hd, bk=bks[j]: e.tensor_copy(out=qT[:, 4 * j:4 * j + 4, hd, :],
                                                                           in_=v4(bk)),
                     reads=[bks[j].rng], writes=q_wr(j, hd))
    for gi in range(4):
        issue_upto(T_ga[gi])
        for cc in range(2):
            hd = 2 * gi + cc
            bks = [nbank(), nbank()]
            fm_mm(T_ga[gi], cc, 16, hT_rhs, hT_main_rngs, bks, [512, 512])
            for j in range(2):
                S.op("act", lambda e, j=j, hd=hd, bk=bks[j]: e.activation(out=sga[:, 4 * j:4 * j + 4, hd, :],
                                                                          in_=v4(bk), func=AF.Silu),
                     reads=[bks[j].rng], writes=[sub(sga, 4 * j * 2048, (4 * j + 4) * 2048)])
    dbg_dump("qT", qT, qT.ap.rearrange("p a b c -> p (a b c)"))
    dbg_dump("kT", kT, kT.ap.rearrange("p a b c -> p (a b c)"))
    dbg_dump("Vt", Vt, Vt.ap.rearrange("p a b -> p (a b)"))
    dbg_dump("sga", sga, sga.ap.rearrange("p a b c -> p (a b c)"))

    if stop == "pb1":
        return finish_early()
    for hh in range(8):
        S.op("dve", lambda e, hh=hh: e.tensor_copy(out=sk_hi[0:1, hh * 128:(hh + 1) * 128],
                                                   in_=expsink[0:1, hh:hh + 1].to_broadcast([1, 128])),
             reads=[expsink.rng], writes=[sub(sk_hi, hh * 256, (hh + 1) * 256)])
    S.op("dve", lambda e: e.tensor_copy(out=sk_f[0:1, :], in_=sk_hi[0:1, :]), reads=[sk_hi.rng], writes=[sk_f.rng])
    for hh in range(8):
        S.op("dve", lambda e, hh=hh: e.tensor_scalar(out=sk_lo[0:1, hh * 128:(hh + 1) * 128],
                                                     in0=sk_f[0:1, hh * 128:(hh + 1) * 128],
                                                     scalar1=-1.0, scalar2=expsink[0:1, hh:hh + 1],
                                                     op0=ALU.mult, op1=ALU.add),
             reads=[sk_f.rng, expsink.rng], writes=[sub(sk_lo, hh * 256, (hh + 1) * 256)])

    its = [(i, kvh) for kvh in range(2) for i in range(8)]

    def pc_geo(it):
        i, kvh = its[it]
        kp = [(i + 1) if i > 0 else 0, i + 2, (i + 3) if i < 7 else 1]
        bi = [3 if i == 0 else 0, 1, 4 if i == 7 else 2]
        sbk = [banks[c_] for c_ in range(3)]
        hs = slice(kvh * 4, kvh * 4 + 4)
        return i, kvh, kp, bi, sbk, hs, PT[it % 2]

    def pc_qk(it):
        i, kvh, kp, bi, sbk, hs, pts = pc_geo(it)
        for c_ in range(3):
            S.op("pe", lambda e, c_=c_, kvh=kvh, i=i, kp=kp, sbk=sbk, hs=hs:
                 e.matmul(sbk[c_].ap, lhsT=kT[:, kp[c_], kvh, :], rhs=qT[:, i, hs, :], start=True, stop=True),
                 reads=[sub(kT, kp[c_] * 512, (kp[c_] + 1) * 512),
                        sub(qT, i * 2048 + kvh * 1024, i * 2048 + (kvh + 1) * 1024)],
                 writes=[sbk[c_].rng])

    def pc_sm(it):
        i, kvh, kp, bi, sbk, hs, pts = pc_geo(it)
        for c_ in range(3):
            S.op("dve", lambda e, c_=c_, sbk=sbk, bi=bi, hs=hs:
                 e.scalar_tensor_tensor(out=scb[c_].ap, in0=v4(sbk[c_]), scalar=QS, in1=biasm[:, bi[c_], hs, :],
                                        op0=ALU.mult, op1=ALU.add),
                 reads=[sbk[c_].rng, biasm.rng], writes=[scb[c_].rng])
            S.op("act", lambda e, c_=c_, pts=pts: e.activation(out=pts[c_].ap, in_=scb[c_].ap, func=AF.Exp),
                 reads=[scb[c_].rng], writes=[pts[c_].rng])

    def pc_pv(it):
        i, kvh, kp, bi, sbk, hs, pts = pc_geo(it)
        obk, dbk = banks[3 + it % 2], banks[5 + it % 2]
        for c_ in range(3):
            S.op("pe", lambda e, c_=c_, kvh=kvh, kp=kp, pts=pts, obk=obk:
                 e.matmul(obk.ap, lhsT=Vt[:, kp[c_], kvh * 128:(kvh + 1) * 128],
                          rhs=pts[c_].ap.rearrange("p a b -> p (a b)"), start=(c_ == 0), stop=(c_ == 2)),
                 reads=[sub(Vt, kp[c_] * 512, (kp[c_] + 1) * 512), pts[c_].rng], writes=[obk.rng])
        for c_ in range(3):
            S.op("pe", lambda e, c_=c_, pts=pts, dbk=dbk:
                 e.matmul(dbk.ap, lhsT=onesb.ap, rhs=pts[c_].ap.rearrange("p a b -> p (a b)"),
                          start=(c_ == 0), stop=False),
                 reads=[onesb.rng, pts[c_].rng], writes=[dbk.rng])
        for j_, skr in enumerate((sk_hi, sk_lo)):
            S.op("pe", lambda e, skr=skr, kvh=kvh, dbk=dbk, j_=j_:
                 e.matmul(dbk.ap, lhsT=onesb[0:1, :], rhs=skr[0:1, kvh * 512:(kvh + 1) * 512],
                          start=False, stop=(j_ == 1)),
                 reads=[onesb.rng, skr.rng], writes=[dbk.rng])

    def pc_fin(it):
        i, kvh, kp, bi, sbk, hs, pts = pc_geo(it)
        obk, dbk = banks[3 + it % 2], banks[5 + it % 2]
        dn = den2[it % 2]
        S.op("act", lambda e, dbk=dbk, dn=dn: e.activation(out=dn.ap, in_=dbk.ap, func=AF.Ln), reads=[dbk.rng],
             writes=[dn.rng])
        S.op("act", lambda e, dn=dn: e.activation(out=dn.ap, in_=dn.ap, func=AF.Exp, scale=-1.0), reads=[dn.rng],
             writes=[dn.rng])
        yrng = sub(yaT, i * 2048 + kvh * 1024, i * 2048 + (kvh + 1) * 1024)
        S.op("dve", lambda e, obk=obk, dn=dn, i=i, hs=hs:
             e.tensor_tensor(out=yaT[:, i, hs, :], in0=v4(obk), in1=dn.ap.rearrange("p (a b) -> p a b", a=4),
                             op=ALU.mult),
             reads=[obk.rng, dn.rng], writes=[yrng])
        S.op("pool", lambda e, i=i, hs=hs: e.tensor_tensor(out=yaT[:, i, hs, :], in0=yaT[:, i, hs, :],
                                                           in1=sga[:, i, hs, :], op=ALU.mult),
             reads=[yrng, sub(sga, i * 2048, (i + 1) * 2048)], writes=[yrng])

    def pc_q(k):
        qi, cc, tt = 2 + k // 4, (k // 2) % 2, k % 2
        if k % 4 == 0:
            issue_upto(T_q[qi])
        hd = 2 * qi + cc
        bk = banks[7]
        fm_mm(T_q[qi], cc, 16, lambda j, kc, tt=tt: hT[:, MAIN[tt], kc, :], [hT_main_rngs[tt]], [bk], [512])
        S.op("dve", lambda e, tt=tt, hd=hd, bk=bk: e.tensor_copy(out=qT[:, 4 * tt:4 * tt + 4, hd, :], in_=v4(bk)),
             reads=[bk.rng], writes=q_wr(tt, hd))

    pc_qk(0)
    pc_sm(0)
    for it in range(16):
        if it < 8:
            pc_q(it)
        if it + 1 < 16:
            pc_qk(it + 1)
        pc_pv(it)
        if it + 1 < 16:
            pc_sm(it + 1)
        pc_fin(it)
    dbg_dump("yaT", yaT, yaT.ap.rearrange("p a b c -> p (a b c)"))

    if stop == "pc":
        return finish_early()
    def halo16(j, kc):
        if j < 2:
            return hT[:, MAIN[j], kc, :]
        st_ = kc * 128 + 120
        return hT_flat[:, st_: st_ + 2 * 1928].rearrange("p (a b) -> p a b", a=2)[:, :, 0:8]

    for g in range(4):
        w_ = 2 ** (g + 1)
        issue_upto(T_u[g])
        for cc in range(2):
            bks = [nbank(), nbank(), nbank()]
            fm_mm(T_u[g], cc, 16, halo16, hT_main_rngs, bks, [512, 512, 16])
            u = ub[cc]
            S.op("dve", lambda e, u=u, bk=bks[0]: e.tensor_copy(out=u[:, 8:520], in_=bk.ap),
                 reads=[bks[0].rng], writes=[u.rng])
            S.op("act", lambda e, u=u, bk=bks[1]: e.activation(out=u[:, 520:1032], in_=bk.ap, func=AF.Copy),
                 reads=[bks[1].rng], writes=[u.rng])
            S.op("dve", lambda e, u=u, bk=bks[2]: e.tensor_scalar(out=u[:, 0:8], in0=bk[:, 0:8],
                                                                   scalar1=pflag[:, 0:1], scalar2=None, op0=ALU.mult),
                 reads=[sub(bks[2], 0, 64), pflag.rng], writes=[u.rng])
            S.op("dve", lambda e, u=u, bk=bks[2]: e.tensor_scalar(out=u[:, 1032:1040], in0=bk[:, 8:16],
                                                                   scalar1=pflag[:, 1:2], scalar2=None, op0=ALU.mult),
                 reads=[sub(bks[2], 0, 64), pflag.rng], writes=[u.rng])
            src = u
            chain = [(1, 1040, 0, 1039, 1, 1040), (2, 1039, 1, 1038, 3, 1040), (4, 1037, 2, 1035, 6, 1039),
                     (8, 1033, 4, 1029, 12, 1037)]
            for lvl in range(g + 1):
                dst = pa if lvl % 2 == 0 else pb
                o0, o1, a0, a1, b0, b1 = chain[lvl]
                S.op("dve", lambda e, dst=dst, src=src, o0=o0, o1=o1, a0=a0, a1=a1, b0=b0, b1=b1:
                     e.tensor_tensor(out=dst[:, o0:o1], in0=src[:, a0:a1], in1=src[:, b0:b1], op=ALU.add),
                     reads=[src.rng], writes=[dst.rng])
                src = dst
            Wb = src
            S.op("dve", lambda e, Wb=Wb, u=u, cc=cc, w_=w_:
                 e.scalar_tensor_tensor(out=pooled[:, cc, :], in0=Wb[:, 8:1032], scalar=1.0 / w_, in1=u[:, 8:1032],
                                        op0=ALU.mult, op1=ALU.subtract),
                 reads=[Wb.rng, u.rng], writes=[sub(pooled, cc * 2048, (cc + 1) * 2048)])
            for (e0, c0_, p0_) in ((8, 0, 0), (1024, 8, 1016)):
                S.op("pool", lambda e, Wb=Wb, e0=e0, c0_=c0_, g=g:
                     e.tensor_tensor(out=etmp[:, 0:8], in0=Wb[:, e0:e0 + 8], in1=cinv[:, g, c0_:c0_ + 8], op=ALU.mult),
                     reads=[Wb.rng, cinv.rng], writes=[etmp.rng])
                S.op("pool", lambda e, u=u, e0=e0, p0_=p0_, cc=cc:
                     e.tensor_tensor(out=pooled[:, cc, p0_:p0_ + 8], in0=etmp[:, 0:8], in1=u[:, e0:e0 + 8],
                                     op=ALU.subtract),
                     reads=[etmp.rng, u.rng], writes=[sub(pooled, cc * 2048, (cc + 1) * 2048)])
        issue_upto(T_gp[g])
        for cc in range(2):
            bks = [nbank(), nbank()]
            fm_mm(T_gp[g], cc, 16, hT_rhs, hT_main_rngs, bks, [512, 512])
            for j in range(2):
                S.op("act", lambda e, j=j, cc=cc, bk=bks[j]: e.activation(out=sgp[:, cc, j * 512:(j + 1) * 512],
                                                                          in_=bk.ap, func=AF.Silu),
                     reads=[bks[j].rng], writes=[sub(sgp, cc * 2048, (cc + 1) * 2048)])
        for dc in range(2):
            for tt in range(2):
                bk = nbank()
                for cch in range(2):
                    S.op("pe", lambda e, g=g, dc=dc, tt=tt, cch=cch, bk=bk:
                         e.matmul(bk.ap, lhsT=wpool[:, g, cch, dc * 128:(dc + 1) * 128],
                                  rhs=pooled[:, cch, tt * 512:(tt + 1) * 512], start=(cch == 0), stop=(cch == 1)),
                         reads=[wpool.rng, sub(pooled, cch * 2048, (cch + 1) * 2048)], writes=[bk.rng])
                kcx = 2 * g + dc
                S.op("dve", lambda e, kcx=kcx, dc=dc, tt=tt, bk=bk:
                     e.scalar_tensor_tensor(out=ypT[:, 4 * tt:4 * tt + 4, kcx, :], in0=v4(bk),
                                            scalar=pscT[:, kcx:kcx + 1],
                                            in1=sgp[:, dc, tt * 512:(tt + 1) * 512].rearrange("p (a b) -> p a b", a=4),
                                            op0=ALU.mult, op1=ALU.mult),
                     reads=[bk.rng, pscT.rng, sub(sgp, dc * 2048, (dc + 1) * 2048)],
                     writes=[sub(ypT, 4 * tt * 2048, (4 * tt + 4) * 2048)])
    dbg_dump("ypT", ypT, ypT.ap.rearrange("p a b c -> p (a b c)"))

    if stop == "pd":
        return finish_early()
    def ya_rhs(j, kc):
        return yaT[:, 4 * j:4 * j + 4, kc, :]

    def yp_rhs(j, kc):
        return ypT[:, 4 * j:4 * j + 4, kc, :]

    ya_rngs = [[sub(yaT, 0, 8192)], [sub(yaT, 8192, 16384)]]
    yp_rngs = [[sub(ypT, 0, 8192)], [sub(ypT, 8192, 16384)]]
    for st in range(8):
        ta, tp, tma, tmp_ = T_e[st]
        issue_upto(tmp_, ta)
        for occ in range(2):
            oc = 2 * st + occ
            fm_mm(tma, occ, 16, hT_rhs, hT_main_rngs, [banks[0], banks[1]], [512, 512])
            fm_mm(tmp_, occ, 16, hT_rhs, hT_main_rngs, [banks[2], banks[3]], [512, 512])
            fm_mm(ta, occ, 8, ya_rhs, ya_rngs, [banks[4], banks[5]], [512, 512])
            fm_mm(tp, occ, 8, yp_rhs, yp_rngs, [banks[6], banks[7]], [512, 512])
            for tt in range(2):
                S.op("act", lambda e, tt=tt, oc=oc: e.activation(out=siga[:, tt * 512:(tt + 1) * 512],
                                                                 in_=banks[tt].ap, func=AF.Sigmoid,
                                                                 bias=bmT[:, oc:oc + 1], scale=1.0),
                     reads=[banks[tt].rng, bmT.rng], writes=[sub(siga, tt * 2048, (tt + 1) * 2048)])
            for tt in range(2):
                S.op("act", lambda e, tt=tt, oc=oc: e.activation(out=sigp[:, tt * 512:(tt + 1) * 512],
                                                                 in_=banks[2 + tt].ap, func=AF.Sigmoid,
                                                                 bias=bmT[:, 16 + oc:17 + oc], scale=1.0),
                     reads=[banks[2 + tt].rng, bmT.rng], writes=[sub(sigp, tt * 2048, (tt + 1) * 2048)])
            for tt in range(2):
                S.op("dve", lambda e, tt=tt: e.tensor_tensor(out=t1[:, tt * 512:(tt + 1) * 512],
                                                             in0=banks[4 + tt].ap,
                                                             in1=siga[:, tt * 512:(tt + 1) * 512], op=ALU.mult),
                     reads=[banks[4 + tt].rng, sub(siga, tt * 2048, (tt + 1) * 2048)],
                     writes=[sub(t1, tt * 2048, (tt + 1) * 2048)])
                S.op("dve", lambda e, tt=tt: e.tensor_tensor(out=t2[:, tt * 512:(tt + 1) * 512],
                                                             in0=banks[6 + tt].ap,
                                                             in1=sigp[:, tt * 512:(tt + 1) * 512], op=ALU.mult),
                     reads=[banks[6 + tt].rng, sub(sigp, tt * 2048, (tt + 1) * 2048)],
                     writes=[sub(t2, tt * 2048, (tt + 1) * 2048)])
                S.op("pool", lambda e, tt=tt, oc=oc:
                     e.tensor_tensor(out=mergedT[:, 4 * tt:4 * tt + 4, oc, :],
                                     in0=t1[:, tt * 512:(tt + 1) * 512].rearrange("p (a b) -> p a b", a=4),
                                     in1=t2[:, tt * 512:(tt + 1) * 512].rearrange("p (a b) -> p a b", a=4), op=ALU.add),
                     reads=[sub(t1, tt * 2048, (tt + 1) * 2048), sub(t2, tt * 2048, (tt + 1) * 2048)],
                     writes=[sub(mergedT, 4 * tt * 4096, (4 * tt + 4) * 4096)])
    dbg_dump("mergedT", mergedT, mergedT.ap.rearrange("p a b c -> p (a b c)"))

    if stop == "pe":
        return finish_early()
    outs = []

    def pf_final(i):
        ob = obuf[i]
        x = xf[i % 2]
        DMA("sp", x.rng, x.ap, xh[128 + 128 * i: 256 + 128 * i, :], name="xf")
        S.op("dve", lambda e, i=i: e.tensor_reduce(out=sst[:, i:i + 1], in_=ssf[:, 4 * i:4 * i + 4], axis=AX.X,
                                                   op=ALU.add),
             reads=[sub(ssf, 16 * i, 16 * i + 16)], writes=[sub(sst, 4 * i, 4 * i + 4)])
        S.op("act", lambda e, i=i: e.activation(out=rstdf[:, i:i + 1], in_=sst[:, i:i + 1], func=AF.Sqrt,
                                                bias=epst[:, 0:1], scale=1.0 / D),
             reads=[sub(sst, 4 * i, 4 * i + 4), epst.rng], writes=[sub(rstdf, 4 * i, 4 * i + 4)])
        S.op("dve", lambda e, i=i: e.reciprocal(out=rstdf[:, i:i + 1], in_=rstdf[:, i:i + 1]),
             reads=[sub(rstdf, 4 * i, 4 * i + 4)], writes=[sub(rstdf, 4 * i, 4 * i + 4)])
        S.op("dve", lambda e, i=i, ob=ob: e.scalar_tensor_tensor(out=ob.ap, in0=ob.ap, scalar=rstdf[:, i:i + 1],
                                                                 in1=modGP.ap, op0=ALU.mult, op1=ALU.mult),
             reads=[ob.rng, sub(rstdf, 4 * i, 4 * i + 4), modGP.rng], writes=[ob.rng])
        S.op("pool" if i % 2 == 0 else "dve",
             lambda e, ob=ob, x=x: e.tensor_tensor(out=ob.ap, in0=ob.ap, in1=x.ap, op=ALU.add),
             reads=[ob.rng, x.rng], writes=[ob.rng])
        outs.append(DMA("sp", None, out_d[128 * i:128 * (i + 1), :], ob.ap, reads=[ob.rng], name="out"))

    issue_upto(T_o[3], T_o[0])
    wos = [tiles[T_o[n]][1] for n in range(4)]

    def pf_evac(i, n, bk):
        ob = obuf[i]
        S.op("dve", lambda e, ob=ob, n=n, bk=bk: e.tensor_copy(out=ob[:, n * 512:(n + 1) * 512], in_=bk.ap),
             reads=[bk.rng], writes=[sub(ob, n * 2048, (n + 1) * 2048)])
        junkF = junkF2[n % 2]
        S.op("dve", lambda e, i=i, n=n, ob=ob, junkF=junkF:
             e.scalar_tensor_tensor(out=junkF.ap, in0=ob[:, n * 512:(n + 1) * 512], scalar=1.0,
                                    in1=ob[:, n * 512:(n + 1) * 512], op0=ALU.mult, op1=ALU.mult,
                                    accum_out=ssf[:, 4 * i + n:4 * i + n + 1]),
             reads=[sub(ob, n * 2048, (n + 1) * 2048)],
             writes=[junkF.rng, sub(ssf, 16 * i + 4 * n, 16 * i + 4 * n + 4)])

    for i in range(8):
        bk = nbank()
        for kc in range(16):
            S.op("pe", lambda e, kc=kc, i=i, bk=bk: e.matmul(bk.ap, lhsT=mergedT[:, i, kc, :], rhs=wos[0][:, kc, :],
                                                             start=(kc == 0), stop=(kc == 15)),
                 reads=[wos[0].rng, sub(mergedT, i * 4096, (i + 1) * 4096)], writes=[bk.rng])
        pf_evac(i, 0, bk)
    for i in range(8):
        bks = [nbank() for _ in range(3)]
        for kc in range(16):
            for n in range(1, 4):
                S.op("pe", lambda e, kc=kc, i=i, n=n, bks=bks: e.matmul(bks[n - 1].ap, lhsT=mergedT[:, i, kc, :],
                                                                         rhs=wos[n][:, kc, :], start=(kc == 0),
                                                                         stop=(kc == 15)),
                     reads=[wos[n].rng, sub(mergedT, i * 4096, (i + 1) * 4096)], writes=[bks[n - 1].rng])
        for n in range(1, 4):
            pf_evac(i, n, bks[n - 1])
        if i >= 1:
            pf_final(i - 1)
    pf_final(7)

    S.emit(nc, es)
    es.close()
    return nc


def _t5_bucket(rel):
    half, max_exact = 16, 8
    ret = np.where(rel > 0, half, 0)
    n = np.abs(rel)
    nf = np.maximum(n, 1).astype(np.float32)
    large = max_exact + (np.log(nf / max_exact) / math.log(128 / max_exact) * (half - max_exact)).astype(np.int32)
    large = np.minimum(large, half - 1)
    return ret + np.where(n < max_exact, n, large)


def _host_inputs(x, c, rel_bias_table, w_ada, b_ada, pre_norm_g, post_norm_g, w_in, attn_sink, w_pool_group,
                 pool_scale, w_branch_attn, w_branch_pool, w_merge, b_merge, w_out):
    f = lambda a: np.ascontiguousarray(np.asarray(a, dtype=np.float32))
    x2 = f(x)[0]
    xpad = np.zeros((S_TOT + 256, D), np.float32)
    xpad[128:128 + S_TOT] = x2
    key = np.arange(128)[:, None]
    qry = np.arange(128)[None, :]
    tab = f(rel_bias_table)
    tiles = []
    for cidx in (-1, 0, 1):
        rel = key + 128 * cidx - qry
        g = tab[_t5_bucket(rel)]
        g = np.transpose(g, (0, 2, 1))
        valid = (np.abs(rel) <= 128)[:, None, :]
        tiles.append(np.where(valid, g, np.float32(NEG)).astype(np.float32))
    allneg = np.full_like(tiles[0], NEG)
    shared = dict(
        cT=f(np.asarray(c)[0].reshape(16, 128).T), w_ada=f(w_ada)[0], b_ada=f(b_ada)[0][None, :],
        pre_g=f(pre_norm_g)[0][None, :], post_g=f(post_norm_g)[0][None, :], w_in=f(w_in)[0],
        w_pool=f(w_pool_group)[0], w_a=f(w_branch_attn)[0], w_p=f(w_branch_pool)[0], w_m=f(w_merge)[0],
        w_out=f(w_out)[0], bmT=f(f(b_merge)[0].reshape(32, 128).T), pscT=f(f(pool_scale)[0].reshape(8, 128).T),
        sinkB=f(np.broadcast_to(f(attn_sink)[0][None, :], (128, 8))), identf=np.eye(128, dtype=np.float32))
    maps = []
    for k in range(NCORES):
        m = dict(shared)
        m["xh"] = np.ascontiguousarray(xpad[k * T: k * T + T + 256])
        first = allneg if k == 0 else tiles[0]
        last = allneg if k == NCORES - 1 else tiles[2]
        m["biasT"] = np.ascontiguousarray(np.stack([tiles[0], tiles[1], tiles[2], first, last], axis=1)
                                          .reshape(128, 5 * 8 * 128))
        m["pflag"] = np.ascontiguousarray(np.broadcast_to(
            np.array([0.0 if k == 0 else 1.0, 0.0 if k == NCORES - 1 else 1.0], np.float32)[None, :], (128, 2)))
        ci = np.zeros((4, 16), np.float32)
        for g in range(4):
            w = 2 ** (g + 1)
            for j in range(16):
                tl = j if j < 8 else T - 16 + j
                gi = k * T + tl
                lo, hi = max(gi - w // 2, 0), min(gi + w // 2, S_TOT)
                ci[g, j] = 1.0 / float(hi - lo)
        m["cinv"] = np.ascontiguousarray(np.broadcast_to(ci.reshape(1, 64), (128, 64)))
        maps.append(m)
    return maps


_NC_CACHE = {}


def kernel(**inputs):
    maps = _host_inputs(**inputs)
    if "nc" not in _NC_CACHE:
        _NC_CACHE["nc"] = build_program()
    nc = _NC_CACHE["nc"]
    res = run_bass_kernel_spmd(nc, maps, core_ids=list(range(NCORES)))
    out = np.concatenate([np.asarray(r["out"], dtype=np.float32) for r in res.results], axis=0)
    return out.reshape(1, S_TOT, D)
```
